# Optimizing a Trainium2 kernel written in Bass

```python
import math
import jax, jax.numpy as jnp
from jax import lax
import numpy as np

D_MODEL = 1024
BATCH = 2
SEQ = 8192
DEPTH = 1

CHUNK = 64
Q_BLOCK = 128
ROPE_THETA = 10000.0
NORM_EPS = 1e-6
SUBLN_EPS = 1e-5

DA_WIDTH = D_MODEL
DA_QK_DIM = 64
DA_V_DIM = 2 * DA_QK_DIM
DA_HEADS = DA_WIDTH // DA_V_DIM

FOX_WIDTH = D_MODEL
FOX_HEAD_DIM = 128
FOX_HEADS = FOX_WIDTH // FOX_HEAD_DIM

IN_SIZES = [DA_WIDTH, DA_WIDTH, DA_WIDTH, DA_WIDTH,
            FOX_WIDTH, FOX_WIDTH, FOX_WIDTH, FOX_WIDTH,
            FOX_HEADS,
            D_MODEL, D_MODEL]
N_IN = sum(IN_SIZES)
IN_SPLITS = [int(v) for v in np.cumsum(IN_SIZES)[:-1]]

kernel_name = "hybrid_diff_fox_gated_block"


def rms_norm(x, g, eps=NORM_EPS):
    xf = x.astype(jnp.float32)
    y = xf * lax.rsqrt(jnp.mean(xf * xf, axis=-1, keepdims=True) + eps)
    return (y * g.astype(jnp.float32)).astype(x.dtype)


def to_heads(t, n):
    b, s, _ = t.shape
    return t.reshape(b, s, n, -1).transpose(0, 2, 1, 3)


def merge_heads(t):
    b, h, s, d = t.shape
    return t.transpose(0, 2, 1, 3).reshape(b, s, h * d)


def rope(t, positions):
    half = t.shape[-1] // 2
    inv_freq = ROPE_THETA ** (-jnp.arange(half, dtype=jnp.float32) / half)
    ang = positions.astype(jnp.float32)[:, None, :, None] * inv_freq
    cos, sin = jnp.cos(ang), jnp.sin(ang)
    tf = t.astype(jnp.float32)
    t1, t2 = tf[..., :half], tf[..., half:]
    return jnp.concatenate([t1 * cos - t2 * sin, t1 * sin + t2 * cos], axis=-1).astype(t.dtype)


def diff_attention(q1, q2, k1, k2, v, lam):
    s_len = q1.shape[2]
    scale = DA_QK_DIM ** -0.5
    outs = []
    for qb in range(s_len // Q_BLOCK):
        q0, kend = qb * Q_BLOCK, (qb + 1) * Q_BLOCK
        mask = (jnp.arange(kend) // CHUNK)[None, :] <= (jnp.arange(q0, kend) // CHUNK)[:, None]
        s1 = jnp.einsum('bhqd,bhkd->bhqk', q1[:, :, q0:kend], k1[:, :, :kend]).astype(jnp.float32) * scale
        s2 = jnp.einsum('bhqd,bhkd->bhqk', q2[:, :, q0:kend], k2[:, :, :kend]).astype(jnp.float32) * scale
        p1 = jax.nn.softmax(jnp.where(mask, s1, -jnp.inf), axis=-1)
        p2 = jax.nn.softmax(jnp.where(mask, s2, -jnp.inf), axis=-1)
        a = (p1 - lam * p2).astype(v.dtype)
        outs.append(jnp.einsum('bhqk,bhkd->bhqd', a, v[:, :, :kend]))
    return jnp.concatenate(outs, axis=2)


def forgetting_attention(q, k, v, cum_logf):
    s_len = q.shape[2]
    scale = FOX_HEAD_DIM ** -0.5
    outs = []
    for qb in range(s_len // Q_BLOCK):
        q0, kend = qb * Q_BLOCK, (qb + 1) * Q_BLOCK
        mask = jnp.arange(kend)[None, :] <= jnp.arange(q0, kend)[:, None]
        s = (jnp.einsum('bhqd,bhkd->bhqk', q[:, :, q0:kend], k[:, :, :kend]).astype(jnp.float32) * scale
             + cum_logf[:, :, q0:kend, None] - cum_logf[:, :, None, :kend])
        p = jax.nn.softmax(jnp.where(mask, s, -jnp.inf), axis=-1).astype(v.dtype)
        outs.append(jnp.einsum('bhqk,bhkd->bhqd', p, v[:, :, :kend]))
    return jnp.concatenate(outs, axis=2)


def setup_inputs(seed: int = 0) -> dict:
    key = jax.random.key(seed)
    ks = jax.random.split(key, 20)
    f32 = jnp.float32
    d = D_MODEL
    x = jax.random.normal(ks[0], (BATCH, SEQ, d), f32)
    c = jax.random.normal(ks[1], (BATCH, d), f32)
    offset = jax.random.randint(ks[2], (BATCH, 1), 0, 4096, dtype=jnp.int32)
    positions = (offset + jnp.arange(SEQ, dtype=jnp.int32)[None, :]).astype(jnp.int32)
    w_ada = jax.random.normal(ks[3], (DEPTH, d, 3 * d), f32) * (0.5 * d ** -0.5)
    b_ada = jax.random.normal(ks[4], (DEPTH, 3 * d), f32) * 0.02
    g_norm = 1.0 + 0.05 * jax.random.normal(ks[5], (DEPTH, d), f32)
    w_in = jax.random.normal(ks[6], (DEPTH, d, N_IN), f32) * d ** -0.5
    b_forget = 2.0 + 0.5 * jax.random.normal(ks[7], (DEPTH, FOX_HEADS), f32)
    lambda_q1 = 0.1 * jax.random.normal(ks[8], (DEPTH, DA_QK_DIM), f32)
    lambda_k1 = 0.1 * jax.random.normal(ks[9], (DEPTH, DA_QK_DIM), f32)
    lambda_q2 = 0.1 * jax.random.normal(ks[10], (DEPTH, DA_QK_DIM), f32)
    lambda_k2 = 0.1 * jax.random.normal(ks[11], (DEPTH, DA_QK_DIM), f32)
    g_subln = 1.0 + 0.05 * jax.random.normal(ks[12], (DEPTH, DA_V_DIM), f32)
    w_branch_a = jax.random.normal(ks[13], (DEPTH, DA_WIDTH, d), f32) * DA_WIDTH ** -0.5
    w_branch_b = jax.random.normal(ks[14], (DEPTH, FOX_WIDTH, d), f32) * FOX_WIDTH ** -0.5
    w_out = jax.random.normal(ks[15], (DEPTH, d, d), f32) * d ** -0.5
    g_final = 1.0 + 0.05 * jax.random.normal(ks[16], (d,), f32)
    return {"x": x, "c": c, "positions": positions, "w_ada": w_ada, "b_ada": b_ada,
            "g_norm": g_norm, "w_in": w_in, "b_forget": b_forget,
            "lambda_q1": lambda_q1, "lambda_k1": lambda_k1, "lambda_q2": lambda_q2,
            "lambda_k2": lambda_k2, "g_subln": g_subln, "w_branch_a": w_branch_a,
            "w_branch_b": w_branch_b, "w_out": w_out, "g_final": g_final}


def reference(x, c, positions, w_ada, b_ada, g_norm, w_in, b_forget, lambda_q1, lambda_k1,
              lambda_q2, lambda_k2, g_subln, w_branch_a, w_branch_b, w_out, g_final):
    c_act = jax.nn.silu(c)
    for l in range(DEPTH):
        lambda_init = 0.8 - 0.6 * math.exp(-0.3 * l)
        mod = jnp.einsum('bd,de->be', c_act, w_ada[l]) + b_ada[l]
        shift, scale, gate = jnp.split(mod, 3, axis=-1)
        h = rms_norm(x, g_norm[l]) * (1.0 + scale[:, None, :]) + shift[:, None, :]

        proj = jnp.einsum('bsd,de->bse', h, w_in[l])
        (q_a, k_a, v_a, z_a, q_b, k_b, v_b, z_b, f_logit, m_a, m_b) = jnp.split(proj, IN_SPLITS, axis=-1)

        qa = to_heads(q_a, DA_HEADS)
        ka = to_heads(k_a, DA_HEADS)
        q1 = rope(qa[..., :DA_QK_DIM], positions)
        q2 = rope(qa[..., DA_QK_DIM:], positions)
        k1 = rope(ka[..., :DA_QK_DIM], positions)
        k2 = rope(ka[..., DA_QK_DIM:], positions)
        va = to_heads(v_a, DA_HEADS)
        lam = (jnp.exp(jnp.sum(lambda_q1[l].astype(jnp.float32) * lambda_k1[l].astype(jnp.float32)))
               - jnp.exp(jnp.sum(lambda_q2[l].astype(jnp.float32) * lambda_k2[l].astype(jnp.float32)))
               + lambda_init)
        oa = diff_attention(q1, q2, k1, k2, va, lam)
        oa = rms_norm(oa, g_subln[l], SUBLN_EPS) * (1.0 - lambda_init)
        y_a = jnp.einsum('bse,ed->bsd', merge_heads(oa) * jax.nn.silu(z_a), w_branch_a[l])

        log_f = jax.nn.log_sigmoid(f_logit.astype(jnp.float32) + b_forget[l].astype(jnp.float32))
        cum_logf = jnp.cumsum(log_f, axis=1).transpose(0, 2, 1)
        ob = forgetting_attention(to_heads(q_b, FOX_HEADS), to_heads(k_b, FOX_HEADS),
                                  to_heads(v_b, FOX_HEADS), cum_logf)
        y_b = jnp.einsum('bse,ed->bsd', merge_heads(ob) * jax.nn.silu(z_b), w_branch_b[l])

        merged = jax.nn.sigmoid(m_a) * y_a + jax.nn.sigmoid(m_b) * y_b
        out = jnp.einsum('bsd,de->bse', merged, w_out[l])
        x = x + gate[:, None, :] * out
    return rms_norm(x, g_final)
```

```python
from contextlib import ExitStack
import math
import numpy as np
import ml_dtypes
import concourse.bass as bass
import concourse.mybir as mybir
from concourse.bass_utils import run_bass_kernel_spmd

F32 = mybir.dt.float32
BF16 = mybir.dt.bfloat16
I32 = mybir.dt.int32
AF = mybir.ActivationFunctionType
ALU = mybir.AluOpType

PE, ACT, DVE, POOL, SP = "tensor", "scalar", "vector", "gpsimd", "sync"
ENGS = [PE, ACT, DVE, POOL, SP]

S_ALL = 8192
NT_ALL = 64
NT_OWN = 16
D = 1024
N_IN = 10248
TWO_PI = 2.0 * math.pi
C1 = 6.28125
C2 = TWO_PI - C1
NEG_BIG = -30000.0


class Op:
    __slots__ = ("idx", "eng", "fn", "deps", "dma_key", "sem", "val", "signal")

    def __init__(self, idx, eng, fn, dma_key):
        self.idx, self.eng, self.fn, self.dma_key = idx, eng, fn, dma_key
        self.deps = set()
        self.sem = None
        self.val = 0
        self.signal = dma_key is not None


class Prog:
    def __init__(self, nc):
        self.nc = nc
        self.ops = []
        self.last_w = {}
        self.readers = {}
        self.barrier_idx = None

    def _key(self, idx):
        od = self.ops[idx]
        return od.dma_key if od.dma_key is not None else ("E", od.eng)

    def op(self, eng, fn, reads=(), writes=(), dma_key=None):
        o = Op(len(self.ops), eng, fn, dma_key)
        deps = {}

        def add(idx):
            k = self._key(idx)
            if deps.get(k, -1) < idx:
                deps[k] = idx

        if self.barrier_idx is not None:
            add(self.barrier_idx)
        for r in reads:
            w = self.last_w.get(r)
            if w is not None:
                add(w)
        for w_ in writes:
            w = self.last_w.get(w_)
            if w is not None:
                add(w)
            for rd in self.readers.get(w_, {}).values():
                add(rd)
        for d in deps.values():
            od = self.ops[d]
            if od.eng == PE and eng == PE and od.dma_key is None and dma_key is None:
                continue
            o.deps.add(d)
            od.signal = True
        me = dma_key if dma_key is not None else ("E", eng)
        for w_ in writes:
            self.last_w[w_] = o.idx
            self.readers[w_] = {}
        for r in reads:
            if r not in writes:
                self.readers.setdefault(r, {})[me] = o.idx
        self.ops.append(o)
        return o

    def dma(self, eng, out, in_, reads=(), writes=(), key=None):
        return self.op(eng, lambda e: e.dma_start(out=out, in_=in_), reads, writes, dma_key=key)

    def barrier(self, eng, fn):
        allres = set(self.last_w.keys()) | set(self.readers.keys())
        o = self.op(eng, fn, reads=(), writes=tuple(allres))
        self.barrier_idx = o.idx
        self.last_w = {}
        self.readers = {}
        return o

    def emit(self, stack, final_deps_res=()):
        nc = self.nc
        sems = {}
        cnt = {}
        fdeps = set()
        for r in final_deps_res:
            w = self.last_w.get(r)
            if w is not None:
                fdeps.add(w)
                self.ops[w].signal = True
        for o in self.ops:
            if not o.signal:
                continue
            name = ("e_" + o.eng) if o.dma_key is None else ("d_" + o.dma_key)
            inc = 1 if o.dma_key is None else 16
            cnt[name] = cnt.get(name, 0) + inc
            o.sem = name
            o.val = cnt[name]
        for name in cnt:
            sems[name] = stack.enter_context(nc.semaphore("s_" + name))
        self.sem_counts = cnt
        per_eng = {e: [] for e in ENGS}
        for o in self.ops:
            per_eng[o.eng].append(o)
        block = stack.enter_context(nc.Block())
        ops = self.ops

        def make(eng_name):
            def body(eng):
                waited = {}
                for o in per_eng[eng_name]:
                    need = {}
                    for d in o.deps:
                        od = ops[d]
                        if od.val > need.get(od.sem, 0):
                            need[od.sem] = od.val
                    for s, v in need.items():
                        if waited.get(s, 0) >= v:
                            continue
                        eng.wait_ge(sems[s], v)
                        waited[s] = v
                    ins = o.fn(eng)
                    if o.signal:
                        ins.then_inc(sems[o.sem], 1 if o.dma_key is None else 16)
                if eng_name == SP:
                    need = {}
                    for d in fdeps:
                        od = ops[d]
                        if od.val > need.get(od.sem, 0):
                            need[od.sem] = od.val
                    for s, v in need.items():
                        if waited.get(s, 0) < v:
                            eng.wait_ge(sems[s], v)
            return body

        for e in ENGS:
            if per_eng[e] or e == SP:
                getattr(block, e)(make(e))


class Arena:
    def __init__(self, t, nbytes):
        self.t = t
        self.nbytes = nbytes
        self.off = 0

    def reset(self):
        self.off = 0

    def alloc(self, shape_free, dt):
        esz = 4 if dt in (F32, I32) else 2
        n = int(np.prod(shape_free))
        nb = n * esz
        nb_al = (nb + 63) // 64 * 64
        assert self.off + nb_al <= self.nbytes, f"arena overflow {self.off + nb_al} > {self.nbytes}"
        a = self.off // 2
        ap = self.t[:, a:a + nb // 2]
        self.off += nb_al
        if dt != BF16:
            ap = ap.bitcast(dt)
        if len(shape_free) == 2:
            ap = ap.rearrange("p (a b) -> p a b", b=shape_free[1])
        elif len(shape_free) == 3:
            ap = ap.rearrange("p (a b c) -> p a b c", b=shape_free[1], c=shape_free[2])
        return ap


def build_program(dbg=False):
    nc = bass.Bass("TRN2", target_bir_lowering=False)

    def din(name, shape, dt=F32):
        return nc.dram_tensor(name, list(shape), dt, kind="ExternalInput").ap()

    x_all = din("x_all", [S_ALL, D])
    x_own = din("x_own", [NT_OWN * 128, D])
    cT_d = din("cT", [128, 8])
    pos_all_d = din("pos_all", [128, NT_ALL], I32)
    pos_own_d = din("pos_own", [128, NT_OWN], I32)
    w_ada = din("w_ada", [D, 3 * D])
    b_ada = din("b_ada", [1, 3 * D])
    g_norm = din("g_norm", [1, D])
    w_in = din("w_in", [D, N_IN])
    b_forget = din("b_forget", [1, 8])
    lq1 = din("lambda_q1", [1, 64])
    lk1 = din("lambda_k1", [1, 64])
    lq2 = din("lambda_q2", [1, 64])
    lk2 = din("lambda_k2", [1, 64])
    g_subln = din("g_subln", [1, 128])
    w_ba = din("w_branch_a", [D, D])
    w_bb = din("w_branch_b", [D, D])
    w_out = din("w_out", [D, D])
    g_final = din("g_final", [1, D])
    invf_d = din("inv_freq", [1, 64])
    maskA_d = din("maskA", [128, 512], BF16)
    maskB_d = din("maskB", [128, 512])
    jsel_d = din("jsel", [1, 4])
    out_d = nc.dram_tensor("out_own", [NT_OWN * 128, D], F32, kind="ExternalOutput").ap()

    sk = "ExternalOutput" if dbg else "Internal"
    kT_s = nc.dram_tensor("kT_s", [16, 128, S_ALL], BF16, kind=sk).ap()
    v_s = nc.dram_tensor("v_s", [16, 128, NT_ALL, 129], BF16, kind=sk).ap()
    qT_s = nc.dram_tensor("qT_s", [16, 128, NT_OWN * 128], BF16, kind=sk).ap()
    sz_s = nc.dram_tensor("sz_s", [16, 128, NT_OWN, 128], BF16, kind=sk).ap()
    sm_s = nc.dram_tensor("sm_s", [NT_OWN, 128, 2048], BF16, kind=sk).ap()
    u_s = nc.dram_tensor("u_s", [NT_OWN, 128, 2048], BF16, kind=sk).ap()
    negL_s = nc.dram_tensor("negL_s", [8, NT_OWN * 128], F32, kind=sk).ap()
    if dbg:
        dbgL = nc.dram_tensor("dbgL", [128, 512 + 128 + 16], F32, kind="ExternalOutput").ap()

    with ExitStack() as st:
        def sb(name, shape, dt):
            return st.enter_context(nc.sbuf_tensor(name, list(shape), dt))

        pairs = [st.enter_context(nc.psum_tensor(f"pp{i}", [128, 1024], F32)) for i in range(4)]
        bk = [pairs[i // 2][:, (i % 2) * 512:(i % 2 + 1) * 512] for i in range(8)]
        bkb = [b.bitcast(BF16) for b in bk]
        BK = [f"bk{i}" for i in range(8)]

        gsB = sb("gsB", [128, D], F32)
        shiftB = sb("shiftB", [128, D], F32)
        gateB = sb("gateB", [128, D], F32)
        gfinB = sb("gfinB", [128, D], F32)
        gsubB = sb("gsubB", [128, 128], F32)
        bfB = sb("bfB", [128, 8], F32)
        lamt = sb("lamt", [128, 8], F32)
        lv = sb("lv", [128, 4, 64], F32)
        identf = sb("identf", [128, 128], F32)
        ident = sb("ident", [128, 128], BF16)
        onesf = sb("onesf", [128, 128], F32)
        trif = sb("trif", [128, 128], F32)
        maskA = sb("maskA_s", [128, 4, 128], BF16)
        maskB = sb("maskB_s", [128, 4, 128], F32)
        jsel = sb("jsel_s", [128, 4], F32)
        invf = sb("invf", [128, 64], F32)
        cos_own = sb("cos_own", [128, NT_OWN, 32], F32)
        sin_own = sb("sin_own", [128, NT_OWN, 32], F32)
        Lall = sb("Lall", [128, NT_ALL, 8], F32)
        Lown = sb("Lown", [128, NT_OWN, 8], F32)
        stat = sb("stat", [128, 4 * 96], F32)
        epsT = sb("epsT", [128, 2], F32)
        ARENA_BYTES = 177 * 1024
        arena_t = sb("arena", [128, ARENA_BYTES // 2], BF16)
        AR = Arena(arena_t, ARENA_BYTES)

        P = Prog(nc)
        uid = [0]

        def R(name):
            uid[0] += 1
            return f"{name}#{uid[0]}"

        def act(out, in_, func, reads, writes, **kw):
            return P.op(ACT, lambda e: e.activation(out=out, in_=in_, func=func, **kw), reads, writes)

        def tt(eng, out, in0, in1, op, reads, writes):
            return P.op(eng, lambda e: e.tensor_tensor(out=out, in0=in0, in1=in1, op=op), reads, writes)

        def ts(eng, out, in0, s1, s2, op0, op1, reads, writes):
            if op1 is None:
                return P.op(eng, lambda e: e.tensor_single_scalar(out=out, in_=in0, scalar=s1, op=op0), reads, writes)
            return P.op(eng, lambda e: e.tensor_scalar(out=out, in0=in0, scalar1=s1, scalar2=s2, op0=op0, op1=op1), reads, writes)

        def stt(eng, out, in0, scalar, in1, op0, op1, reads, writes, accum_out=None):
            if accum_out is None:
                return P.op(eng, lambda e: e.scalar_tensor_tensor(out=out, in0=in0, scalar=scalar, in1=in1, op0=op0, op1=op1), reads, writes)
            return P.op(eng, lambda e: e.scalar_tensor_tensor(out=out, in0=in0, scalar=scalar, in1=in1, op0=op0, op1=op1, accum_out=accum_out), reads, writes)

        def cp(eng, out, in_, reads, writes):
            if eng == ACT:
                return P.op(eng, lambda e: e.activation(out=out, in_=in_, func=AF.Copy), reads, writes)
            return P.op(eng, lambda e: e.tensor_copy(out=out, in_=in_), reads, writes)

        def mm(out, lhsT, rhs, start, stop, reads, writes, skip=False):
            if skip:
                return P.op(PE, lambda e: e.matmul(out, lhsT=lhsT, rhs=rhs, start=start, stop=stop, skip_group_check=True), reads, writes)
            return P.op(PE, lambda e: e.matmul(out, lhsT=lhsT, rhs=rhs, start=start, stop=stop), reads, writes)

        def tr(out, in_, reads, writes, idn=None):
            idn = ident if idn is None else idn
            return P.op(PE, lambda e: e.transpose(out=out, in_=in_, identity=idn[:]), list(reads) + ["ident"], writes)

        TOP = (ARENA_BYTES - (8 * 4096 * 2 + 8 * 8 * 2 + 64 + 8 * 8 * 4 + 4 * 4096)) // 64 * 64
        AR.off = TOP
        wkv = AR.alloc([8, 4096], BF16)
        wf = AR.alloc([8, 8], BF16)
        wfst = AR.alloc([8, 8], F32)
        wst_top = [AR.alloc([1024], F32) for _ in range(4)]
        AR.reset()
        c_sb = AR.alloc([8], F32)
        c_sig = AR.alloc([8], F32)
        c_act = AR.alloc([8], F32)
        bada_r = AR.alloc([3 * D], F32)
        gn_r = AR.alloc([D], F32)
        mod_r = AR.alloc([3 * D], F32)
        gs_r = AR.alloc([D], F32)
        wada = [AR.alloc([3 * D], F32) for _ in range(2)]
        posi = AR.alloc([NT_ALL], I32)
        posf = AR.alloc([NT_ALL], F32)
        posoi = AR.alloc([NT_OWN], I32)
        posof = AR.alloc([NT_OWN], F32)
        ang = AR.alloc([32, 32], F32)
        rk = AR.alloc([32, 32], F32)
        rki = AR.alloc([32, 32], I32)
        rr = AR.alloc([32, 32], F32)
        rm = AR.alloc([32, 32], F32)
        phase0_end = AR.off

        def load_bc(dst, src, n, name):
            P.dma(SP, dst, src.partition_broadcast(128), writes=[name], key=R("ld"))

        P.dma(SP, c_sb, cT_d, writes=["c_sb"], key=R("ld"))
        P.dma(SP, bada_r[0:1, :], b_ada, writes=["bada_r"], key=R("ld"))
        P.dma(SP, gn_r[0:1, :], g_norm, writes=["gn_r"], key=R("ld"))
        load_bc(gfinB[:], g_final, D, "gfinB")
        load_bc(gsubB[:], g_subln, 128, "gsubB")
        load_bc(bfB[:], b_forget, 8, "bfB")
        for n_, src in enumerate((lq1, lk1, lq2, lk2)):
            load_bc(lv[:, n_, :], src, 64, f"lv{n_}")
        load_bc(invf[:], invf_d, 64, "invf")
        load_bc(jsel[:], jsel_d, 4, "jsel")
        P.dma(SP, maskA[:].rearrange("p a b -> p (a b)"), maskA_d, writes=["maskA"], key=R("ld"))
        P.dma(SP, maskB[:].rearrange("p a b -> p (a b)"), maskB_d, writes=["maskB"], key=R("ld"))
        P.dma(SP, posi, pos_all_d, writes=["Aposi"], key=R("ld"))
        P.dma(SP, posoi, pos_own_d, writes=["Oposi"], key=R("ld"))

        P.op(POOL, lambda e: e.memset(identf[:], 0.0), writes=["identf"])
        P.op(POOL, lambda e: e.affine_select(out=identf[:], in_=identf[:], pattern=[[-1, 128]], compare_op=ALU.not_equal, fill=1.0, base=0, channel_multiplier=1), reads=["identf"], writes=["identf"])
        P.op(POOL, lambda e: e.memset(trif[:], 1.0), writes=["trif"])
        P.op(POOL, lambda e: e.affine_select(out=trif[:], in_=trif[:], pattern=[[1, 128]], compare_op=ALU.is_ge, fill=0.0, base=0, channel_multiplier=-1), reads=["trif"], writes=["trif"])
        P.op(POOL, lambda e: e.memset(onesf[:], 1.0), writes=["onesf"])
        P.op(POOL, lambda e: e.memset(epsT[:, 0:1], 1e-6), writes=["epsT"])
        P.op(POOL, lambda e: e.memset(epsT[:, 1:2], 1e-5), writes=["epsT"])
        cp(DVE, ident[:], identf[:], ["identf"], ["ident"])

        act(c_sig, c_sb, AF.Sigmoid, ["c_sb"], ["c_sig"])
        tt(DVE, c_act, c_sb, c_sig, ALU.mult, ["c_sb", "c_sig"], ["c_act"])
        for ch in range(8):
            sl = ch % 2
            P.dma(SP, wada[sl], w_ada[ch * 128:(ch + 1) * 128, :], writes=[f"wada{sl}"], key=f"wada{sl}")
            for g in range(6):
                mm(bk[g][0:1, :], c_act[:, ch:ch + 1], wada[sl][:, g * 512:(g + 1) * 512], ch == 0, ch == 7,
                   ["c_act", f"wada{sl}"], [BK[g]])
        for g in range(6):
            tt(DVE, mod_r[0:1, g * 512:(g + 1) * 512], bk[g][0:1, :], bada_r[0:1, g * 512:(g + 1) * 512], ALU.add,
               [BK[g], "bada_r"], [f"mod{g}"])
        for hf in range(2):
            stt(DVE, gs_r[0:1, hf * 512:(hf + 1) * 512], mod_r[0:1, D + hf * 512:D + (hf + 1) * 512], 1.0,
                gn_r[0:1, hf * 512:(hf + 1) * 512], ALU.add, ALU.mult, [f"mod{2 + hf}", "gn_r"], [f"gs_r{hf}"])
        bi = 0
        for (row, off, dst, rn) in ((gs_r, 0, gsB, "gs_r"), (mod_r, 0, shiftB, "mod"), (mod_r, 2 * D, gateB, "mod")):
            for hf in range(2):
                b = 6 + (bi % 2)
                bi += 1
                rname = f"gs_r{hf}" if rn == "gs_r" else f"mod{(off // 512) + hf}"
                mm(bk[b][:, :], onesf[0:1, :], row[0:1, off + hf * 512:off + (hf + 1) * 512], True, True,
                   ["onesf", rname], [BK[b]])
                cp(ACT if hf else DVE, dst[:, hf * 512:(hf + 1) * 512], bk[b][:, :], [BK[b]], [dst.tensor.name + str(hf)] if False else [f"{id(dst)}_{hf}"])
        RES_GS = [f"{id(gsB)}_0", f"{id(gsB)}_1"]
        RES_SH = [f"{id(shiftB)}_0", f"{id(shiftB)}_1"]
        RES_GT = [f"{id(gateB)}_0", f"{id(gateB)}_1"]
        for n_ in range(2):
            stt(DVE, lv[:, 2 * n_, :], lv[:, 2 * n_, :], 1.0, lv[:, 2 * n_ + 1, :], ALU.mult, ALU.mult,
                [f"lv{2 * n_}", f"lv{2 * n_ + 1}"], [f"lv{2 * n_}", f"lamt{2 + n_}"], accum_out=lamt[:, 2 + n_:3 + n_])
            act(lamt[:, 4 + n_:5 + n_], lamt[:, 2 + n_:3 + n_], AF.Exp, [f"lamt{2 + n_}"], [f"lamt{4 + n_}"])
        tt(DVE, lamt[:, 0:1], lamt[:, 4:5], lamt[:, 5:6], ALU.subtract, ["lamt4", "lamt5"], ["lam"])
        ts(DVE, lamt[:, 0:1], lamt[:, 0:1], 0.2, None, ALU.add, None, ["lam"], ["lam"])
        ts(DVE, lamt[:, 1:2], lamt[:, 0:1], -1.0, None, ALU.mult, None, ["lam"], ["neglam"])

        wstn_ = 0
        for (col0_, dst0_) in ((1024, 0), (5120, 2048)):
            for ch_ in range(8):
                for c0_ in (0, 1024):
                    sl_ = wstn_ % 4
                    ceng_ = POOL if wstn_ % 4 == 3 else DVE
                    wstn_ += 1
                    P.dma(SP, wst_top[sl_], w_in[ch_ * 128:(ch_ + 1) * 128, col0_ + c0_:col0_ + c0_ + 1024],
                          writes=[f"wstT{sl_}"], key=f"wstT{sl_}")
                    P.op(ceng_, lambda e, o_=wkv[:, ch_, dst0_ + c0_:dst0_ + c0_ + 1024], i_=wst_top[sl_]: e.tensor_copy(out=o_, in_=i_),
                         [f"wstT{sl_}"], ["wkv"])
        P.dma(SP, wfst, w_in[:, 8192:8200].rearrange("(c p) n -> p c n", p=128), writes=["wfst"], key="wfst")
        P.op(POOL, lambda e: e.tensor_copy(out=wf, in_=wfst), ["wfst"], ["wf"])

        def rope_tables(pos_i, pos_f, nt, cos_dst, sin_dst, tag, posres):
            cp(DVE, pos_f, pos_i, [posres], [tag + "posf"])
            a_ = ang[:, 0:nt, :]
            k_ = rk[:, 0:nt, :]
            ki_ = rki[:, 0:nt, :]
            r_ = rr[:, 0:nt, :]
            m_ = rm[:, 0:nt, :]
            tt(DVE, a_, pos_f.unsqueeze(2).to_broadcast([128, nt, 32]), invf[:, 0:32].unsqueeze(1).to_broadcast([128, nt, 32]),
               ALU.mult, [tag + "posf", "invf"], ["ang"])
            tt(DVE, m_, pos_f.unsqueeze(2).to_broadcast([128, nt, 32]), invf[:, 32:64].unsqueeze(1).to_broadcast([128, nt, 32]),
               ALU.mult, [tag + "posf", "invf"], ["rm"])
            tt(DVE, a_, a_, m_, ALU.add, ["ang", "rm"], ["ang"])
            for which, dst in ((0, sin_dst), (1, cos_dst)):
                src = a_
                if which == 1:
                    ts(DVE, r_, a_, math.pi / 2, None, ALU.add, None, ["ang"], ["rr"])
                    src = r_
                ts(DVE, k_, src, 1.0 / TWO_PI, None, ALU.mult, None, ["ang", "rr"], ["rk"])
                cp(DVE, ki_, k_, ["rk"], ["rki"])
                cp(DVE, k_, ki_, ["rki"], ["rk"])
                stt(DVE, m_, k_, -C1, src, ALU.mult, ALU.add, ["rk", "ang", "rr"], ["rm"])
                stt(DVE, r_, k_, -C2, m_, ALU.mult, ALU.add, ["rk", "rm"], ["rr"])
                ts(DVE, m_, r_, math.pi, -TWO_PI, ALU.is_gt, ALU.mult, ["rr"], ["rm"])
                tt(DVE, r_, r_, m_, ALU.add, ["rr", "rm"], ["rr"])
                ts(DVE, m_, r_, -math.pi, TWO_PI, ALU.is_lt, ALU.mult, ["rr"], ["rm"])
                tt(DVE, r_, r_, m_, ALU.add, ["rr", "rm"], ["rr"])
                ts(DVE, r_, r_, math.pi, -math.pi, ALU.min, ALU.max, ["rr"], ["rr"])
                act(dst, r_, AF.Sin, ["rr"], [tag + ("cos" if which else "sin")])

        AR.off = phase0_end
        cos_all = AR.alloc([NT_ALL, 32], F32)
        sin_all = AR.alloc([NT_ALL, 32], F32)
        for hfi in range(2):
            sl32 = slice(hfi * 32, (hfi + 1) * 32)
            rope_tables(posi[:, sl32], posf[:, sl32], 32, cos_all[:, sl32, :], sin_all[:, sl32, :], "A", "Aposi")
        rope_tables(posoi, posof, NT_OWN, cos_own[:], sin_own[:], "O", "Oposi")
        assert AR.off <= TOP, (AR.off, TOP)
        P.barrier(POOL, lambda e: e.memset(stat[:, 0:1], 0.0))
        AR.reset()
        cos_all2 = AR.alloc([NT_ALL, 32], F32)
        sin_all2 = AR.alloc([NT_ALL, 32], F32)
        cp(DVE, cos_all2, cos_all, [], ["cosA"])
        cp(DVE, sin_all2, sin_all, [], ["sinA"])
        P.barrier(POOL, lambda e: e.memset(stat[:, 0:1], 0.0))
        cos_all, sin_all = cos_all2, sin_all2
        wst = None
        xt = [AR.alloc([D], F32) for _ in range(3)]
        junk = AR.alloc([D], BF16)
        hn = AR.alloc([D], F32)
        hb = [AR.alloc([D], BF16) for _ in range(3)]
        hT = [AR.alloc([8, 512], BF16) for _ in range(2)]
        ra = AR.alloc([512], F32)
        rb = AR.alloc([512], F32)
        kar = AR.alloc([D], BF16)
        kTa_st = AR.alloc([8, 512], BF16)
        kTb_st = AR.alloc([8, 512], BF16)
        v_st = [AR.alloc([16, 129], BF16) for _ in range(2)]
        ylog = AR.alloc([NT_ALL, 8], F32)
        lsb = AR.alloc([NT_ALL, 8], F32)
        cumt = AR.alloc([NT_ALL, 8], F32)
        totb = AR.alloc([NT_ALL, 8], F32)
        offs = AR.alloc([NT_ALL, 8], F32)
        assert AR.off <= TOP, (AR.off, TOP)

        wst_n = [0]

        def load_weight_cols(dst_fn, src, col0, ncols, reads_tag):
            for ch in range(8):
                for c0 in range(0, ncols, 1024):
                    n = min(1024, ncols - c0)
                    sl = wst_n[0] % len(wst)
                    ceng = (DVE, ACT, POOL, DVE, ACT)[wst_n[0] % 5]
                    wst_n[0] += 1
                    P.dma(SP, wst[sl][:, 0:n], src[ch * 128:(ch + 1) * 128, col0 + c0:col0 + c0 + n],
                          writes=[f"wst{sl}"], key=f"wst{sl}")
                    cp(ceng, dst_fn(ch, c0, n), wst[sl][:, 0:n], [f"wst{sl}"], [reads_tag])

        for s_ in range(2):
            P.op(POOL, lambda e, s_=s_: e.memset(v_st[s_][:, :, 128:129], 1.0), writes=[f"v_st{s_}"])

        gen_rot = [0]

        def gen_bank():
            b = 5 + (gen_rot[0] % 3)
            gen_rot[0] += 1
            return b

        def norm_tile(x_src_ap, xs, hs, statcol, shift_add=True):
            P.dma(SP, xt[xs], x_src_ap, writes=[f"xt{xs}"], key=f"xt{xs}")
            sc = stat[:, statcol:statcol + 1]
            act(junk, xt[xs], AF.Square, [f"xt{xs}"], ["junk", f"statc{xs}"], accum_out=sc)
            act(sc, sc, AF.Sqrt, [f"statc{xs}"], [f"statc{xs}"], scale=1.0 / D, bias=epsT[:, 0:1])
            P.op(DVE, lambda e: e.reciprocal(out=sc, in_=sc), [f"statc{xs}"], [f"statc{xs}"])
            stt(DVE, hn, xt[xs], sc, gsB[:], ALU.mult, ALU.mult, [f"xt{xs}", f"statc{xs}"] + RES_GS, ["hn"])
            tt(DVE, hb[hs], hn, shiftB[:], ALU.add, ["hn"] + RES_SH, [f"hb{hs}"])

        def transpose_to(dst3, src2, nchunks, src_res, dst_res, bank):
            for c in range(nchunks):
                tr(bkb[bank][:, c * 128:(c + 1) * 128], src2[:, c * 128:(c + 1) * 128], [src_res], [BK[bank]])
            act(dst3, bkb[bank][:, 0:nchunks * 128].rearrange("p (c t) -> p c t", t=128), AF.Copy, [BK[bank]], [dst_res])

        def rope_bank(bank, cos_t, sin_t, dst2, dst_res):
            v = bk[bank][:, :].rearrange("p (b h j) -> p b h j", h=2, j=32)
            t1 = v[:, :, 0, :]
            t2 = v[:, :, 1, :]
            cb = cos_t.unsqueeze(1).to_broadcast([128, 8, 32])
            sbb = sin_t.unsqueeze(1).to_broadcast([128, 8, 32])
            ra3 = ra[:, 0:256].rearrange("p (b j) -> p b j", j=32)
            rb3 = rb[:, 0:256].rearrange("p (b j) -> p b j", j=32)
            d4 = dst2.rearrange("p (b h j) -> p b h j", h=2, j=32)
            tt(DVE, ra3, t1, cb, ALU.mult, [BK[bank], "cs"], ["ra"])
            tt(DVE, rb3, t2, sbb, ALU.mult, [BK[bank], "cs"], ["rb"])
            tt(DVE, d4[:, :, 0, :], ra3, rb3, ALU.subtract, ["ra", "rb"], [dst_res])
            tt(DVE, ra3, t1, sbb, ALU.mult, [BK[bank], "cs"], ["ra"])
            tt(DVE, rb3, t2, cb, ALU.mult, [BK[bank], "cs"], ["rb"])
            tt(DVE, d4[:, :, 1, :], ra3, rb3, ALU.add, ["ra", "rb"], [dst_res])

        def stageN_all(t):
            norm_tile(x_all[t * 128:(t + 1) * 128, :], t % 3, t % 3, t)

        def stageA_all(t):
            g, t4 = t // 4, t % 4
            gs_ = g % 2
            transpose_to(hT[gs_][:, :, t4 * 128:(t4 + 1) * 128], hb[t % 3], 8, f"hb{t % 3}", f"hT{gs_}_{t4}", 0)

        def stageB_all(t):
            g, t4 = t // 4, t % 4
            gs_ = g % 2
            lhs = [hT[gs_][:, c, t4 * 128:(t4 + 1) * 128] for c in range(8)]
            for hf in range(2):
                b = 2 + hf
                for c in range(8):
                    mm(bk[b][:, :], lhs[c], wkv[:, c, hf * 512:(hf + 1) * 512], c == 0, c == 7, [f"hT{gs_}_{t4}", "wkv"], [BK[b]])
                rope_bank(b, cos_all[:, t, :], sin_all[:, t, :], kar[:, hf * 512:(hf + 1) * 512], "kar")
            for c in range(8):
                mm(bk[1][:, 0:8], lhs[c], wf[:, c, :], c == 0, c == 7, [f"hT{gs_}_{t4}", "wf"], [BK[1]])
            tt(DVE, ylog[:, t, :], bk[1][:, 0:8], bfB[:], ALU.add, [BK[1], "bfB"], ["ylog"])
            vs_ = t % 2
            for br, col0 in ((0, 1024), (1, 3072)):
                for hf in range(2):
                    b = gen_bank()
                    for c in range(8):
                        mm(bk[b][:, :], lhs[c], wkv[:, c, col0 + hf * 512:col0 + (hf + 1) * 512], c == 0, c == 7,
                           [f"hT{gs_}_{t4}", "wkv"], [BK[b]])
                    h0 = br * 8 + hf * 4
                    act(v_st[vs_][:, h0:h0 + 4, 0:128], bk[b][:, :].rearrange("p (h c) -> p h c", c=128), AF.Copy,
                        [BK[b]], [f"v_st{vs_}"])
            P.dma(POOL, v_s[:, :, t, :].rearrange("h p c -> p h c"), v_st[vs_], reads=[f"v_st{vs_}"], writes=["v_s"], key=f"v_st{vs_}")

        def stageC_all(t):
            g, t4 = t // 4, t % 4
            gs_ = g % 2
            transpose_to(kTa_st[:, :, t4 * 128:(t4 + 1) * 128], kar, 8, "kar", "kTa_st", 4)
            if t4 != 3:
                return
            for h in range(8):
                b = gen_bank()
                for c in range(8):
                    mm(bk[b][:, :], wkv[:, c, 2048 + h * 128:2048 + (h + 1) * 128], hT[gs_][:, c, :], c == 0, c == 7,
                       [f"hT{gs_}_0", f"hT{gs_}_1", f"hT{gs_}_2", f"hT{gs_}_3", "wkv"], [BK[b]])
                act(kTb_st[:, h, :], bk[b][:, :], AF.Copy, [BK[b]], ["kTb_st"])
            P.dma(POOL, kT_s[0:8, :, g * 512:(g + 1) * 512].rearrange("h p t -> p h t"), kTa_st, reads=["kTa_st"], writes=["kT_s"], key="kTa_st")
            P.dma(POOL, kT_s[8:16, :, g * 512:(g + 1) * 512].rearrange("h p t -> p h t"), kTb_st, reads=["kTb_st"], writes=["kT_s"], key="kTb_st")

        stageN_all(0)
        stageN_all(1)
        stageA_all(0)
        for t in range(NT_ALL):
            if t + 2 < NT_ALL:
                stageN_all(t + 2)
            if t + 1 < NT_ALL:
                stageA_all(t + 1)
            stageB_all(t)
            stageC_all(t)

        yl2 = ylog.rearrange("p a b -> p (a b)")
        ls2 = lsb.rearrange("p a b -> p (a b)")
        act(ls2, yl2, AF.Exp, ["ylog"], ["lsb"], scale=-1.0)
        ts(DVE, ls2, ls2, 1.0, None, ALU.add, None, ["lsb"], ["lsb"])
        act(ls2, ls2, AF.Ln, ["lsb"], ["lsb"])
        mm(bk[2][:, :], trif[:], ls2, True, True, ["trif", "lsb"], [BK[2]])
        mm(bk[3][:, :], onesf[:], ls2, True, True, ["onesf", "lsb"], [BK[3]])
        cp(DVE, cumt.rearrange("p a b -> p (a b)"), bk[2][:, :], [BK[2]], ["cumt"])
        cp(DVE, totb.rearrange("p a b -> p (a b)"), bk[3][:, :], [BK[3]], ["totb"])
        P.op(DVE, lambda e: e.memset(offs[:, 0, :], 0.0), writes=["offs"])
        for t in range(1, NT_ALL):
            tt(DVE, offs[:, t, :], offs[:, t - 1, :], totb[:, t - 1, :], ALU.add, ["offs", "totb"], ["offs"])
        tt(DVE, Lall[:], cumt, offs, ALU.add, ["cumt", "offs"], ["Lall"])
        L4 = Lall[:].rearrange("p (i d) h -> p i d h", d=4)
        ts(DVE, Lown[:], L4[:, :, 0, :], jsel[:, 0:1], None, ALU.mult, None, ["Lall", "jsel"], ["Lown"])
        for d_ in range(1, 4):
            stt(DVE, Lown[:], L4[:, :, d_, :], jsel[:, d_:d_ + 1], Lown[:], ALU.mult, ALU.add, ["Lall", "jsel", "Lown"], ["Lown"])
        mm(bk[4][:, 0:128], Lown[:].rearrange("p a b -> p (a b)"), identf[:], True, True, ["Lown", "identf"], [BK[4]])
        ts(DVE, ra[:, 0:128], bk[4][:, 0:128], -1.0, None, ALU.mult, None, [BK[4]], ["ra"])
        for i in range(NT_OWN):
            P.dma(POOL, negL_s[:, i * 128:(i + 1) * 128], ra[i * 8:(i + 1) * 8, 0:128], reads=["ra"], writes=["negL_s"], key=R("st"))
        if dbg:
            P.dma(POOL, dbgL[:, 0:512], Lall[:].rearrange("p a b -> p (a b)"), reads=["Lall"], key=R("st"))
            P.dma(POOL, dbgL[:, 512:640], Lown[:].rearrange("p a b -> p (a b)"), reads=["Lown"], key=R("st"))
            P.dma(POOL, dbgL[:, 640:642], lamt[:, 0:2], reads=["lam", "neglam"], key=R("st"))

        P.barrier(POOL, lambda e: e.memset(stat[:, 0:1], 0.0))
        AR.reset()
        wown = AR.alloc([8, 3072], BF16)
        wst = [AR.alloc([1024], F32) for _ in range(8)]
        xt = [AR.alloc([D], F32) for _ in range(3)]
        junk = AR.alloc([D], BF16)
        hn = AR.alloc([D], F32)
        hb = [AR.alloc([D], BF16) for _ in range(3)]
        hTo = AR.alloc([8, NT_OWN * 128], BF16)
        ra = AR.alloc([512], F32)
        rb = AR.alloc([512], F32)
        qar = AR.alloc([D], BF16)
        qTa_st = AR.alloc([8, 512], BF16)
        qTb_st = AR.alloc([8, 512], BF16)
        sgt = [AR.alloc([512], F32) for _ in range(2)]
        o_st = [AR.alloc([2048], BF16) for _ in range(2)]

        load_weight_cols(lambda ch, c0, n: wown[:, ch, c0:c0 + n], w_in, 0, 1024, "wown")
        load_weight_cols(lambda ch, c0, n: wown[:, ch, 1024 + c0:1024 + c0 + n], w_in, 4096, 1024, "wown")
        load_weight_cols(lambda ch, c0, n: wown[:, ch, 2048 + c0:2048 + c0 + n], w_in, 3072, 1024, "wown")
        osn = [0]

        def silu_out(col0, i, head0):
            os_ = osn[0] % 2
            osn[0] += 1
            for hf in range(2):
                b = gen_bank()
                for c in range(8):
                    mm(bk[b][:, :], hTo[:, c, i * 128:(i + 1) * 128], wown[:, c, col0 + hf * 512:col0 + (hf + 1) * 512],
                       c == 0, c == 7, [f"hTo{i}", "wown"], [BK[b]])
                act(sgt[hf], bk[b][:, :], AF.Sigmoid, [BK[b]], [f"sgt{hf}"])
                tt(DVE, o_st[os_][:, hf * 512:(hf + 1) * 512], bk[b][:, :], sgt[hf], ALU.mult, [BK[b], f"sgt{hf}"], [f"o_st{os_}"])
            P.dma(POOL, sz_s[head0:head0 + 8, :, i, :].rearrange("h p c -> p h c"),
                  o_st[os_][:, 0:1024].rearrange("p (h c) -> p h c", c=128), reads=[f"o_st{os_}"], writes=["sz_s"], key=f"o_st{os_}")

        def stageN_own(i):
            norm_tile(x_own[i * 128:(i + 1) * 128, :], i % 3, i % 3, 64 + i)

        def stageA_own(i):
            transpose_to(hTo[:, :, i * 128:(i + 1) * 128], hb[i % 3], 8, f"hb{i % 3}", f"hTo{i}", 0)

        def stageB_own(i):
            lhs = [hTo[:, c, i * 128:(i + 1) * 128] for c in range(8)]
            for hf in range(2):
                b = 2 + hf
                for c in range(8):
                    mm(bk[b][:, :], lhs[c], wown[:, c, hf * 512:(hf + 1) * 512], c == 0, c == 7, [f"hTo{i}", "wown"], [BK[b]])
                rope_bank(b, cos_own[:, i, :], sin_own[:, i, :], qar[:, hf * 512:(hf + 1) * 512], "qar")
            silu_out(2048, i, 0)

        def stageC_own(i):
            g, t4 = i // 4, i % 4
            transpose_to(qTa_st[:, :, t4 * 128:(t4 + 1) * 128], qar, 8, "qar", "qTa_st", 4)
            if t4 != 3:
                return
            for h in range(8):
                b = gen_bank()
                for c in range(8):
                    mm(bk[b][:, :], wown[:, c, 1024 + h * 128:1024 + (h + 1) * 128], hTo[:, c, g * 512:(g + 1) * 512], c == 0, c == 7,
                       [f"hTo{4 * g}", f"hTo{4 * g + 1}", f"hTo{4 * g + 2}", f"hTo{4 * g + 3}", "wown"], [BK[b]])
                act(qTb_st[:, h, :], bk[b][:, :], AF.Copy, [BK[b]], ["qTb_st"])
            P.dma(POOL, qT_s[0:8, :, g * 512:(g + 1) * 512].rearrange("h p t -> p h t"), qTa_st, reads=["qTa_st"], writes=["qT_s"], key="qTa_st")
            P.dma(POOL, qT_s[8:16, :, g * 512:(g + 1) * 512].rearrange("h p t -> p h t"), qTb_st, reads=["qTb_st"], writes=["qT_s"], key="qTb_st")

        stageN_own(0)
        stageN_own(1)
        stageN_own(2)
        stageA_own(0)
        stageA_own(1)
        for i in range(NT_OWN):
            if i + 3 < NT_OWN:
                stageN_own(i + 3)
            if i + 2 < NT_OWN:
                stageA_own(i + 2)
            stageB_own(i)
            stageC_own(i)

        load_weight_cols(lambda ch, c0, n: wown[:, ch, c0:c0 + n], w_in, 7168, 1024, "wown")
        load_weight_cols(lambda ch, c0, n: wown[:, ch, 1024 + c0:1024 + c0 + n], w_in, 8200, 2048, "wown")
        for i in range(NT_OWN):
            silu_out(0, i, 8)
            os_ = osn[0] % 2
            osn[0] += 1
            for q4 in range(4):
                b = gen_bank()
                for c in range(8):
                    mm(bk[b][:, :], hTo[:, c, i * 128:(i + 1) * 128], wown[:, c, 1024 + q4 * 512:1024 + (q4 + 1) * 512],
                       c == 0, c == 7, [f"hTo{i}", "wown"], [BK[b]])
                act(o_st[os_][:, q4 * 512:(q4 + 1) * 512], bk[b][:, :], AF.Sigmoid, [BK[b]], [f"o_st{os_}"])
            P.dma(POOL, sm_s[i], o_st[os_], reads=[f"o_st{os_}"], writes=["sm_s"], key=f"o_st{os_}")

        P.barrier(POOL, lambda e: e.memset(stat[:, 0:1], 0.0))
        AR.reset()
        Kb = [AR.alloc([S_ALL], BF16) for _ in range(2)]
        Vb = [AR.alloc([NT_ALL, 129], BF16) for _ in range(2)]
        Qb = [AR.alloc([NT_OWN * 128], BF16) for _ in range(2)]
        SZb = [AR.alloc([NT_OWN, 128], BF16) for _ in range(2)]
        NPB = 7
        LAG = 5
        Pb = [AR.alloc([2, 512], BF16) for _ in range(NPB)]
        tmpb = [AR.alloc([512], F32) for _ in range(5)]
        FqB = [AR.alloc([NT_OWN * 128], F32) for _ in range(2)]
        dg = [AR.alloc([128], F32) for _ in range(2)]
        usth = [AR.alloc([NT_OWN, 128], BF16) for _ in range(2)]
        obuf2 = [AR.alloc([NT_OWN, 128], F32) for _ in range(2)]
        accS = AR.alloc([8, 129], F32)
        eo1 = AR.alloc([128], F32)
        ejk = AR.alloc([128], F32)
        est = AR.alloc([16], F32)
        mscol2 = [AR.alloc([NT_OWN], F32) for _ in range(2)]

        SCALE_A = 64 ** -0.5
        SCALE_B = 128 ** -0.5

        def build_fq(h8):
            fs = h8 % 2
            P.dma(SP, FqB[fs], negL_s[h8:h8 + 1, :].partition_broadcast(128), reads=["negL_s"], writes=[f"FqB{fs}"], key=f"FqB{fs}")

        steps = []
        ccn = 0
        for hh in range(16):
            for c in range(4):
                nk = 16 * c + 16
                for k in range(nk):
                    steps.append((hh, c, k, ccn, k == 0, k == nk - 1))
                ccn += 1

        def head_loads(hh):
            hs = hh % 2
            P.dma(SP, Kb[hs], kT_s[hh], reads=["kT_s"], writes=[f"K{hs}"], key=f"K{hs}")
            P.dma(SP, Vb[hs], v_s[hh], reads=["v_s"], writes=[f"V{hs}"], key=f"V{hs}")
            P.dma(SP, Qb[hs], qT_s[hh], reads=["qT_s"], writes=[f"Q{hs}"], key=f"Q{hs}")
            P.dma(SP, SZb[hs], sz_s[hh], reads=["sz_s"], writes=[f"SZ{hs}"], key=f"SZ{hs}")
            if hh == 7:
                build_fq(0)
            elif 8 <= hh < 15:
                build_fq(hh - 8 + 1)

        def amin_of(c, k):
            return max(0, -(-(k - 16 * c - 3) // 4))

        def front(n):
            hh, c, k, cc, first, last = steps[n]
            is_a = hh < 8
            hs = hh % 2
            h8 = hh % 8
            a_min = amin_of(c, k)
            col0 = a_min * 128
            ps = n % NPB
            if is_a:
                sp = n % 2
                s3 = pairs[sp][:, :].rearrange("p (m q) -> p m q", m=2)
                for m in range(2):
                    pr = slice(m * 64, (m + 1) * 64)
                    mm(bk[2 * sp + m][:, col0:512], Kb[hs][pr, k * 128:(k + 1) * 128],
                       Qb[hs][pr, c * 512 + col0:(c + 1) * 512], True, True, [f"K{hs}", f"Q{hs}"], [BK[2 * sp + m]])
                act(Pb[ps][:, :, col0:512], s3[:, :, col0:512], AF.Exp, [BK[2 * sp], BK[2 * sp + 1]], [f"P{ps}"], scale=SCALE_A)
                for a in range(a_min, 4):
                    d_ = k - 16 * c - 4 * a
                    if 0 <= d_ <= 3:
                        blk = slice(a * 128, (a + 1) * 128)
                        tt(POOL, Pb[ps][:, :, blk], Pb[ps][:, :, blk], maskA[:, d_, :].unsqueeze(1).to_broadcast([128, 2, 128]),
                           ALU.mult, [f"P{ps}", "maskA"], [f"P{ps}"])
            else:
                sbk = n % 4
                ts_ = n % 5
                fs = h8 % 2
                mm(bk[sbk][:, col0:512], Kb[hs][:, k * 128:(k + 1) * 128], Qb[hs][:, c * 512 + col0:(c + 1) * 512], True, True,
                   [f"K{hs}", f"Q{hs}"], [BK[sbk]])
                stt(DVE, tmpb[ts_][:, col0:512], bk[sbk][:, col0:512], SCALE_B, FqB[fs][:, c * 512 + col0:(c + 1) * 512],
                    ALU.mult, ALU.add, [BK[sbk], f"FqB{fs}"], [f"tmp{ts_}"])
                for a in range(a_min, 4):
                    d_ = k - 16 * c - 4 * a
                    if 0 <= d_ <= 3:
                        blk = slice(a * 128, (a + 1) * 128)
                        tt(POOL, tmpb[ts_][:, blk], tmpb[ts_][:, blk], maskB[:, d_, :], ALU.add, [f"tmp{ts_}", "maskB"], [f"tmp{ts_}"])
                act(Pb[ps][:, 0, col0:512], tmpb[ts_][:, col0:512], AF.Exp, [f"tmp{ts_}", "Lall"], [f"P{ps}"],
                    bias=Lall[:, k, h8:h8 + 1], scale=1.0)

        def acc_a(m, a):
            b = 4 + 2 * m + a // 2
            o_ = (a % 2) * 129
            return b, bk[b][:, o_:o_ + 129]

        def acc_b(cc, a):
            b = 4 + 2 * (cc % 2) + a // 2
            o_ = (a % 2) * 129
            return b, bk[b][:, o_:o_ + 129]

        def zero_banks(banks):
            for b in banks:
                P.op(DVE, lambda e, b=b: e.memset(bk[b][:, 0:258], 0.0), writes=[BK[b]])

        def next_is_a(n):
            return n + 1 < NS and steps[n + 1][0] < 8

        def back(n):
            hh, c, k, cc, first, last = steps[n]
            is_a = hh < 8
            hs = hh % 2
            a_min = amin_of(c, k)
            ps = n % NPB
            if n == 0:
                zero_banks((4, 5, 6, 7))
            if is_a:
                obuf, mscol = obuf2[hs], mscol2[hs]
                OB, MS = f"obuf{hs}", f"mscol{hs}"
                for a in range(a_min, 4):
                    blk = slice(a * 128, (a + 1) * 128)
                    for m in range(2):
                        b, acc = acc_a(m, a)
                        mm(acc, Pb[ps][:, m, blk], Vb[hs][:, k, :], False, False, [f"P{ps}", f"V{hs}"], [BK[b]], skip=True)
                if not last:
                    return
                for j, b in enumerate((4, 5, 6, 7)):
                    src = bk[b][:, 0:258].rearrange("p (a c) -> p a c", c=129)
                    cp(DVE, accS[:, 2 * j:2 * j + 2, :], src, [BK[b]], [f"accS{j}"])
                if n + 1 < NS:
                    zero_banks((4, 5, 6, 7))
                for a in range(4):
                    i = 4 * c + a
                    j1, j2 = a // 2, 2 + a // 2
                    acc1 = accS[:, 2 * j1 + a % 2, :]
                    acc2 = accS[:, 2 * j2 + a % 2, :]
                    P.op(DVE, lambda e, acc1=acc1: e.reciprocal(out=est[:, 0:1], in_=acc1[:, 128:129]), [f"accS{j1}"], ["est0"])
                    P.op(DVE, lambda e, acc2=acc2: e.reciprocal(out=est[:, 1:2], in_=acc2[:, 128:129]), [f"accS{j2}"], ["est1"])
                    tt(DVE, est[:, 1:2], est[:, 1:2], lamt[:, 1:2], ALU.mult, ["est1", "neglam"], ["est1"])
                    ts(DVE, eo1, acc1[:, 0:128], est[:, 0:1], None, ALU.mult, None, [f"accS{j1}", "est0"], ["eo1"])
                    stt(DVE, obuf[:, i, :], acc2[:, 0:128], est[:, 1:2], eo1, ALU.mult, ALU.add, [f"accS{j2}", "est1", "eo1"], [OB])
                    stt(DVE, ejk, obuf[:, i, :], 1.0, obuf[:, i, :], ALU.mult, ALU.mult, [OB], ["ejk", MS], accum_out=mscol[:, i:i + 1])
                if c != 3:
                    return

                def head_epi(hh=hh, hs=hs, obuf=obuf, mscol=mscol, OB=OB, MS=MS):
                    act(mscol, mscol, AF.Sqrt, [MS], [MS], scale=1.0 / 128, bias=epsT[:, 1:2])
                    P.op(DVE, lambda e: e.reciprocal(out=mscol, in_=mscol), [MS], [MS])
                    ts(DVE, mscol, mscol, 0.8, None, ALU.mult, None, [MS], [MS])
                    tt(DVE, obuf, obuf, mscol.unsqueeze(2).to_broadcast([128, NT_OWN, 128]), ALU.mult, [OB, MS], [OB])
                    tt(DVE, obuf, obuf, gsubB[:].unsqueeze(1).to_broadcast([128, NT_OWN, 128]), ALU.mult, [OB, "gsubB"], [OB])
                    tt(DVE, usth[hs], obuf, SZb[hs], ALU.mult, [OB, f"SZ{hs}"], [f"usth{hs}"])
                    P.dma(POOL, u_s[:, :, hh * 128:(hh + 1) * 128].rearrange("i p c -> p i c"), usth[hs],
                          reads=[f"usth{hs}"], writes=["u_s"], key=f"usth{hs}")
                deferred.append((n + 28, head_epi))
                return
            for a in range(a_min, 4):
                blk = slice(a * 128, (a + 1) * 128)
                b, acc = acc_b(cc, a)
                mm(acc, Pb[ps][:, 0, blk], Vb[hs][:, k, :], False, False, [f"P{ps}", f"V{hs}"], [BK[b]], skip=True)
            if not last:
                return
            if n + 1 < NS:
                sn_ = (cc + 1) % 2
                zero_banks((4 + 2 * sn_, 5 + 2 * sn_))
            for a in range(4):
                i = 4 * c + a
                b1, acc1 = acc_b(cc, a)
                P.op(DVE, lambda e, acc1=acc1: e.reciprocal(out=est[:, 0:1], in_=acc1[:, 128:129]), [BK[b1]], ["est0"])
                stt(DVE, usth[hs][:, i, :], acc1[:, 0:128], est[:, 0:1], SZb[hs][:, i, :], ALU.mult, ALU.mult,
                    [BK[b1], "est0", f"SZ{hs}"], [f"usth{hs}"])
            if c != 3:
                return
            P.dma(POOL, u_s[:, :, hh * 128:(hh + 1) * 128].rearrange("i p c -> p i c"), usth[hs],
                  reads=[f"usth{hs}"], writes=["u_s"], key=f"usth{hs}")

        NS = len(steps)
        loaded = -1
        deferred = []
        for n in range(NS + LAG):
            if n < NS:
                hh = steps[n][0]
                if hh > loaded:
                    head_loads(hh)
                    loaded = hh
                front(n)
            if n - LAG >= 0:
                back(n - LAG)
                while deferred and deferred[0][0] <= n - LAG:
                    deferred.pop(0)[1]()
        while deferred:
            deferred.pop(0)[1]()

        P.barrier(POOL, lambda e: e.memset(stat[:, 0:1], 0.0))
        AR.reset()
        wba = AR.alloc([8, D], BF16)
        wbb = AR.alloc([8, D], BF16)
        wo = AR.alloc([8, D], BF16)
        wst = [AR.alloc([1024], F32) for _ in range(8)]
        Ub = [AR.alloc([2048], BF16) for _ in range(2)]
        SMb = [AR.alloc([2048], BF16) for _ in range(2)]
        Xb = [AR.alloc([D], F32) for _ in range(2)]
        uT = [AR.alloc([16, 128], BF16) for _ in range(2)]
        t1 = AR.alloc([D], F32)
        t2 = AR.alloc([D], F32)
        t3 = AR.alloc([D], F32)
        mg = AR.alloc([D], BF16)
        mT = AR.alloc([8, 128], BF16)
        xn = AR.alloc([D], F32)
        junk = AR.alloc([D], BF16)
        Ob = [AR.alloc([D], F32) for _ in range(2)]
        load_weight_cols(lambda ch, c0, n: wba[:, ch, c0:c0 + n], w_ba, 0, 1024, "wba")
        load_weight_cols(lambda ch, c0, n: wbb[:, ch, c0:c0 + n], w_bb, 0, 1024, "wbb")
        load_weight_cols(lambda ch, c0, n: wo[:, ch, c0:c0 + n], w_out, 0, 1024, "wo")
        def p3_loads(i):
            s_ = i % 2
            P.dma(SP, Ub[s_], u_s[i], reads=["u_s"], writes=[f"U{s_}"], key=f"U{s_}")
            P.dma(SP, SMb[s_], sm_s[i], reads=["sm_s"], writes=[f"SM{s_}"], key=f"SM{s_}")
            P.dma(SP, Xb[s_], x_own[i * 128:(i + 1) * 128, :], writes=[f"X{s_}"], key=f"X{s_}")

        def p3_A(i):
            s_ = i % 2
            p3_loads(i)
            transpose_to(uT[s_][:, 0:8, :], Ub[s_][:, 0:1024], 8, f"U{s_}", f"uTa{s_}", 0)
            transpose_to(uT[s_][:, 8:16, :], Ub[s_][:, 1024:2048], 8, f"U{s_}", f"uTb{s_}", 1)

        def p3_B(i, hf):
            s_ = i % 2
            hsl = slice(hf * 512, (hf + 1) * 512)
            ba, bb = 2 + 2 * hf, 3 + 2 * hf
            for c in range(8):
                mm(bk[ba][:, :], uT[s_][:, c, :], wba[:, c, hsl], c == 0, c == 7, [f"uTa{s_}", "wba"], [BK[ba]])
            for c in range(8):
                mm(bk[bb][:, :], uT[s_][:, 8 + c, :], wbb[:, c, hsl], c == 0, c == 7, [f"uTb{s_}", "wbb"], [BK[bb]])
            tt(DVE, t1[:, hsl], bk[ba][:, :], SMb[s_][:, hsl], ALU.mult, [BK[ba], f"SM{s_}"], [f"t1{hf}"])
            tt(DVE, t2[:, hsl], bk[bb][:, :], SMb[s_][:, 1024 + hf * 512:1024 + (hf + 1) * 512], ALU.mult, [BK[bb], f"SM{s_}"], [f"t2{hf}"])
            tt(POOL, mg[:, hsl], t1[:, hsl], t2[:, hsl], ALU.add, [f"t1{hf}", f"t2{hf}"], [f"mg{hf}"])

        def p3_D(i):
            s_ = i % 2
            for c in range(8):
                tr(bkb[6][:, c * 128:(c + 1) * 128], mg[:, c * 128:(c + 1) * 128], [f"mg{c // 4}"], [BK[6]])
            act(mT, bkb[6][:, 0:1024].rearrange("p (c t) -> p c t", t=128), AF.Copy, [BK[6]], ["mT"])
            for hf in range(2):
                b = 7 if hf == 0 else 6
                hsl = slice(hf * 512, (hf + 1) * 512)
                for c in range(8):
                    mm(bk[b][:, :], mT[:, c, :], wo[:, c, hsl], c == 0, c == 7, ["mT", "wo"], [BK[b]])
                tt(DVE, t3[:, hsl], bk[b][:, :], gateB[:, hsl], ALU.mult, [BK[b]] + RES_GT, [f"t3{hf}"])
                tt(POOL, xn[:, hsl], t3[:, hsl], Xb[s_][:, hsl], ALU.add, [f"t3{hf}", f"X{s_}"], [f"xn{hf}"])
            sc = stat[:, 96 + i:97 + i]
            act(junk, xn, AF.Square, ["xn0", "xn1"], ["junk", "statc"], accum_out=sc)
            act(sc, sc, AF.Sqrt, ["statc"], ["statc"], scale=1.0 / D, bias=epsT[:, 0:1])
            P.op(DVE, lambda e, sc=sc: e.reciprocal(out=sc, in_=sc), ["statc"], ["statc"])
            stt(DVE, Ob[s_], xn, sc, gfinB[:], ALU.mult, ALU.mult, ["xn0", "xn1", "statc", "gfinB"], [f"O{s_}"])
            P.dma(POOL, out_d[i * 128:(i + 1) * 128, :], Ob[s_], reads=[f"O{s_}"], writes=[f"out{i}"], key=f"O{s_}")

        p3_A(0)
        for i in range(NT_OWN):
            if i + 1 < NT_OWN:
                p3_A(i + 1)
            p3_B(i, 0)
            p3_B(i, 1)
            p3_D(i)

        P.emit(st, final_deps_res=[f"out{i}" for i in range(NT_OWN)] + (["kT_s", "v_s", "qT_s", "sz_s", "sm_s", "u_s"] if dbg else []))
    return nc, P


_CACHE = {}


def _host_inputs(x, c, positions, w_ada, b_ada, g_norm, w_in, b_forget, lambda_q1, lambda_k1,
                 lambda_q2, lambda_k2, g_subln, w_branch_a, w_branch_b, w_out, g_final):
    f32 = np.float32
    inv64 = 10000.0 ** (-(np.arange(32, dtype=np.float64) / 32.0))
    inv_hi = inv64.astype(f32)
    inv_lo = (inv64 - inv_hi.astype(np.float64)).astype(f32)
    inv_freq = np.concatenate([inv_hi, inv_lo]).reshape(1, 64)
    ins = []
    s_i = np.arange(128)[:, None]
    t_i = np.arange(128)[None, :]
    for core in range(8):
        b, j = core // 4, core % 4
        xb = np.ascontiguousarray(x[b], dtype=f32)
        own_tiles = np.arange(NT_OWN) * 4 + j
        x_own = np.ascontiguousarray(xb.reshape(NT_ALL, 128, D)[own_tiles].reshape(NT_OWN * 128, D))
        pos_b = np.asarray(positions[b], dtype=np.int32).reshape(NT_ALL, 128)
        mA = np.zeros((128, 4, 128), f32)
        mB = np.full((128, 4, 128), NEG_BIG, f32)
        for d_ in range(4):
            if d_ < j:
                mA[:, d_, :] = 1.0
                mB[:, d_, :] = 0.0
            elif d_ == j:
                mA[:, d_, :] = ((s_i // 64) <= (t_i // 64)).astype(f32)
                mB[:, d_, :] = np.where(s_i <= t_i, 0.0, NEG_BIG).astype(f32)
        jsel = np.zeros((1, 4), f32)
        jsel[0, j] = 1.0
        ins.append({
            "x_all": xb, "x_own": x_own,
            "cT": np.ascontiguousarray(np.asarray(c[b], f32).reshape(8, 128).T),
            "pos_all": np.ascontiguousarray(pos_b.T), "pos_own": np.ascontiguousarray(pos_b[own_tiles].T),
            "w_ada": np.ascontiguousarray(w_ada[0], f32), "b_ada": np.ascontiguousarray(b_ada[0], f32).reshape(1, -1),
            "g_norm": np.asarray(g_norm[0], f32).reshape(1, -1), "w_in": np.ascontiguousarray(w_in[0], f32),
            "b_forget": np.asarray(b_forget[0], f32).reshape(1, -1),
            "lambda_q1": np.asarray(lambda_q1[0], f32).reshape(1, -1), "lambda_k1": np.asarray(lambda_k1[0], f32).reshape(1, -1),
            "lambda_q2": np.asarray(lambda_q2[0], f32).reshape(1, -1), "lambda_k2": np.asarray(lambda_k2[0], f32).reshape(1, -1),
            "g_subln": np.asarray(g_subln[0], f32).reshape(1, -1),
            "w_branch_a": np.ascontiguousarray(w_branch_a[0], f32), "w_branch_b": np.ascontiguousarray(w_branch_b[0], f32),
            "w_out": np.ascontiguousarray(w_out[0], f32), "g_final": np.asarray(g_final, f32).reshape(1, -1),
            "inv_freq": inv_freq,
            "maskA": mA.reshape(128, 512).astype(ml_dtypes.bfloat16), "maskB": mB.reshape(128, 512),
            "jsel": jsel,
        })
    return ins


def kernel(**inputs):
    inputs = {k: np.asarray(v) for k, v in inputs.items()}
    if "nc" not in _CACHE:
        _CACHE["nc"] = build_program(False)[0]
    nc = _CACHE["nc"]
    ins = _host_inputs(**inputs)
    res = run_bass_kernel_spmd(nc, ins, core_ids=list(range(8)))
    out = np.zeros((2, S_ALL, D), np.float32)
    o4 = out.reshape(2, NT_ALL, 128, D)
    for core in range(8):
        b, j = core // 4, core % 4
        y = np.asarray(res.results[core]["out_own"], dtype=np.float32).reshape(NT_OWN, 128, D)
        o4[b, np.arange(NT_OWN) * 4 + j] = y
    return out
```

```python
from contextlib import ExitStack
import math
import numpy as np
import ml_dtypes
import concourse.bass as bass
import concourse.mybir as mybir
from concourse.bass_utils import run_bass_kernel_spmd

F32 = mybir.dt.float32
BF16 = mybir.dt.bfloat16
I32 = mybir.dt.int32
AF = mybir.ActivationFunctionType
ALU = mybir.AluOpType

PE, ACT, DVE, POOL, SP = "tensor", "scalar", "vector", "gpsimd", "sync"
ENGS = [PE, ACT, DVE, POOL, SP]

S_ALL = 8192
NT_ALL = 64
NT_OWN = 16
D = 1024
N_IN = 10248
TWO_PI = 2.0 * math.pi
C1 = 6.28125
C2 = TWO_PI - C1
NEG_BIG = -30000.0


class Op:
    __slots__ = ("idx", "eng", "fn", "deps", "dma_key", "sem", "val", "signal")

    def __init__(self, idx, eng, fn, dma_key):
        self.idx, self.eng, self.fn, self.dma_key = idx, eng, fn, dma_key
        self.deps = set()
        self.sem = None
        self.val = 0
        self.signal = dma_key is not None


class Prog:
    def __init__(self, nc):
        self.nc = nc
        self.ops = []
        self.last_w = {}
        self.readers = {}
        self.barrier_idx = None

    def _key(self, idx):
        od = self.ops[idx]
        return od.dma_key if od.dma_key is not None else ("E", od.eng)

    def op(self, eng, fn, reads=(), writes=(), dma_key=None):
        o = Op(len(self.ops), eng, fn, dma_key)
        deps = {}

        def add(idx):
            k = self._key(idx)
            if deps.get(k, -1) < idx:
                deps[k] = idx

        if self.barrier_idx is not None:
            add(self.barrier_idx)
        for r in reads:
            w = self.last_w.get(r)
            if w is not None:
                add(w)
        for w_ in writes:
            w = self.last_w.get(w_)
            if w is not None:
                add(w)
            for rd in self.readers.get(w_, {}).values():
                add(rd)
        for d in deps.values():
            od = self.ops[d]
            if od.eng == PE and eng == PE and od.dma_key is None and dma_key is None:
                continue
            o.deps.add(d)
            od.signal = True
        me = dma_key if dma_key is not None else ("E", eng)
        for w_ in writes:
            self.last_w[w_] = o.idx
            self.readers[w_] = {}
        for r in reads:
            if r not in writes:
                self.readers.setdefault(r, {})[me] = o.idx
        self.ops.append(o)
        return o

    def dma(self, eng, out, in_, reads=(), writes=(), key=None):
        return self.op(eng, lambda e: e.dma_start(out=out, in_=in_), reads, writes, dma_key=key)

    def barrier(self, eng, fn):
        allres = set(self.last_w.keys()) | set(self.readers.keys())
        o = self.op(eng, fn, reads=(), writes=tuple(allres))
        self.barrier_idx = o.idx
        self.last_w = {}
        self.readers = {}
        return o

    def emit(self, stack, final_deps_res=()):
        nc = self.nc
        sems = {}
        cnt = {}
        fdeps = set()
        for r in final_deps_res:
            w = self.last_w.get(r)
            if w is not None:
                fdeps.add(w)
                self.ops[w].signal = True
        for o in self.ops:
            if not o.signal:
                continue
            name = ("e_" + o.eng) if o.dma_key is None else ("d_" + o.dma_key)
            inc = 1 if o.dma_key is None else 16
            cnt[name] = cnt.get(name, 0) + inc
            o.sem = name
            o.val = cnt[name]
        for name in cnt:
            sems[name] = stack.enter_context(nc.semaphore("s_" + name))
        self.sem_counts = cnt
        per_eng = {e: [] for e in ENGS}
        for o in self.ops:
            per_eng[o.eng].append(o)
        block = stack.enter_context(nc.Block())
        ops = self.ops

        def make(eng_name):
            def body(eng):
                waited = {}
                for o in per_eng[eng_name]:
                    need = {}
                    for d in o.deps:
                        od = ops[d]
                        if od.val > need.get(od.sem, 0):
                            need[od.sem] = od.val
                    for s, v in need.items():
                        if waited.get(s, 0) >= v:
                            continue
                        eng.wait_ge(sems[s], v)
                        waited[s] = v
                    ins = o.fn(eng)
                    if o.signal:
                        ins.then_inc(sems[o.sem], 1 if o.dma_key is None else 16)
                if eng_name == SP:
                    need = {}
                    for d in fdeps:
                        od = ops[d]
                        if od.val > need.get(od.sem, 0):
                            need[od.sem] = od.val
                    for s, v in need.items():
                        if waited.get(s, 0) < v:
                            eng.wait_ge(sems[s], v)
            return body

        for e in ENGS:
            if per_eng[e] or e == SP:
                getattr(block, e)(make(e))


class Arena:
    def __init__(self, t, nbytes):
        self.t = t
        self.nbytes = nbytes
        self.off = 0

    def reset(self):
        self.off = 0

    def alloc(self, shape_free, dt):
        esz = 4 if dt in (F32, I32) else 2
        n = int(np.prod(shape_free))
        nb = n * esz
        nb_al = (nb + 63) // 64 * 64
        assert self.off + nb_al <= self.nbytes, f"arena overflow {self.off + nb_al} > {self.nbytes}"
        a = self.off // 2
        ap = self.t[:, a:a + nb // 2]
        self.off += nb_al
        if dt != BF16:
            ap = ap.bitcast(dt)
        if len(shape_free) == 2:
            ap = ap.rearrange("p (a b) -> p a b", b=shape_free[1])
        elif len(shape_free) == 3:
            ap = ap.rearrange("p (a b c) -> p a b c", b=shape_free[1], c=shape_free[2])
        return ap


def build_program(dbg=False):
    nc = bass.Bass("TRN2", target_bir_lowering=False)

    def din(name, shape, dt=F32):
        return nc.dram_tensor(name, list(shape), dt, kind="ExternalInput").ap()

    x_all = din("x_all", [S_ALL, D])
    x_own = din("x_own", [NT_OWN * 128, D])
    cT_d = din("cT", [128, 8])
    pos_all_d = din("pos_all", [128, NT_ALL], I32)
    pos_own_d = din("pos_own", [128, NT_OWN], I32)
    w_ada = din("w_ada", [D, 3 * D])
    b_ada = din("b_ada", [1, 3 * D])
    g_norm = din("g_norm", [1, D])
    w_in = din("w_in", [D, N_IN])
    b_forget = din("b_forget", [1, 8])
    lq1 = din("lambda_q1", [1, 64])
    lk1 = din("lambda_k1", [1, 64])
    lq2 = din("lambda_q2", [1, 64])
    lk2 = din("lambda_k2", [1, 64])
    g_subln = din("g_subln", [1, 128])
    w_ba = din("w_branch_a", [D, D])
    w_bb = din("w_branch_b", [D, D])
    w_out = din("w_out", [D, D])
    g_final = din("g_final", [1, D])
    invf_d = din("inv_freq", [1, 64])
    maskA_d = din("maskA", [128, 512], BF16)
    maskB_d = din("maskB", [128, 512])
    jsel_d = din("jsel", [1, 4])
    out_d = nc.dram_tensor("out_own", [NT_OWN * 128, D], F32, kind="ExternalOutput").ap()

    sk = "ExternalOutput" if dbg else "Internal"
    kT_s = nc.dram_tensor("kT_s", [16, 128, S_ALL], BF16, kind=sk).ap()
    v_s = nc.dram_tensor("v_s", [16, 128, NT_ALL, 129], BF16, kind=sk).ap()
    qT_s = nc.dram_tensor("qT_s", [16, 128, NT_OWN * 128], BF16, kind=sk).ap()
    sz_s = nc.dram_tensor("sz_s", [16, 128, NT_OWN, 128], BF16, kind=sk).ap()
    sm_s = nc.dram_tensor("sm_s", [NT_OWN, 128, 2048], BF16, kind=sk).ap()
    u_s = nc.dram_tensor("u_s", [NT_OWN, 128, 2048], BF16, kind=sk).ap()
    negL_s = nc.dram_tensor("negL_s", [8, NT_OWN * 128], F32, kind=sk).ap()
    if dbg:
        dbgL = nc.dram_tensor("dbgL", [128, 512 + 128 + 16], F32, kind="ExternalOutput").ap()

    with ExitStack() as st:
        def sb(name, shape, dt):
            return st.enter_context(nc.sbuf_tensor(name, list(shape), dt))

        pairs = [st.enter_context(nc.psum_tensor(f"pp{i}", [128, 1024], F32)) for i in range(4)]
        bk = [pairs[i // 2][:, (i % 2) * 512:(i % 2 + 1) * 512] for i in range(8)]
        bkb = [b.bitcast(BF16) for b in bk]
        BK = [f"bk{i}" for i in range(8)]

        gsB = sb("gsB", [128, D], F32)
        shiftB = sb("shiftB", [128, D], F32)
        gateB = sb("gateB", [128, D], F32)
        gfinB = sb("gfinB", [128, D], F32)
        gsubB = sb("gsubB", [128, 128], F32)
        bfB = sb("bfB", [128, 8], F32)
        lamt = sb("lamt", [128, 8], F32)
        lv = sb("lv", [128, 4, 64], F32)
        identf = sb("identf", [128, 128], F32)
        ident = sb("ident", [128, 128], BF16)
        onesf = sb("onesf", [128, 128], F32)
        trif = sb("trif", [128, 128], F32)
        maskA = sb("maskA_s", [128, 4, 128], BF16)
        maskB = sb("maskB_s", [128, 4, 128], F32)
        jsel = sb("jsel_s", [128, 4], F32)
        invf = sb("invf", [128, 64], F32)
        cos_own = sb("cos_own", [128, NT_OWN, 32], F32)
        sin_own = sb("sin_own", [128, NT_OWN, 32], F32)
        Lall = sb("Lall", [128, NT_ALL, 8], F32)
        Lown = sb("Lown", [128, NT_OWN, 8], F32)
        stat = sb("stat", [128, 4 * 96], F32)
        epsT = sb("epsT", [128, 2], F32)
        ARENA_BYTES = 177 * 1024
        arena_t = sb("arena", [128, ARENA_BYTES // 2], BF16)
        AR = Arena(arena_t, ARENA_BYTES)

        P = Prog(nc)
        uid = [0]

        def R(name):
            uid[0] += 1
            return f"{name}#{uid[0]}"

        def act(out, in_, func, reads, writes, **kw):
            return P.op(ACT, lambda e: e.activation(out=out, in_=in_, func=func, **kw), reads, writes)

        def tt(eng, out, in0, in1, op, reads, writes):
            return P.op(eng, lambda e: e.tensor_tensor(out=out, in0=in0, in1=in1, op=op), reads, writes)

        def ts(eng, out, in0, s1, s2, op0, op1, reads, writes):
            if op1 is None:
                return P.op(eng, lambda e: e.tensor_single_scalar(out=out, in_=in0, scalar=s1, op=op0), reads, writes)
            return P.op(eng, lambda e: e.tensor_scalar(out=out, in0=in0, scalar1=s1, scalar2=s2, op0=op0, op1=op1), reads, writes)

        def stt(eng, out, in0, scalar, in1, op0, op1, reads, writes, accum_out=None):
            if accum_out is None:
                return P.op(eng, lambda e: e.scalar_tensor_tensor(out=out, in0=in0, scalar=scalar, in1=in1, op0=op0, op1=op1), reads, writes)
            return P.op(eng, lambda e: e.scalar_tensor_tensor(out=out, in0=in0, scalar=scalar, in1=in1, op0=op0, op1=op1, accum_out=accum_out), reads, writes)

        def cp(eng, out, in_, reads, writes):
            if eng == ACT:
                return P.op(eng, lambda e: e.activation(out=out, in_=in_, func=AF.Copy), reads, writes)
            return P.op(eng, lambda e: e.tensor_copy(out=out, in_=in_), reads, writes)

        def mm(out, lhsT, rhs, start, stop, reads, writes, skip=False):
            if skip:
                return P.op(PE, lambda e: e.matmul(out, lhsT=lhsT, rhs=rhs, start=start, stop=stop, skip_group_check=True), reads, writes)
            return P.op(PE, lambda e: e.matmul(out, lhsT=lhsT, rhs=rhs, start=start, stop=stop), reads, writes)

        def tr(out, in_, reads, writes, idn=None):
            idn = ident if idn is None else idn
            return P.op(PE, lambda e: e.transpose(out=out, in_=in_, identity=idn[:]), list(reads) + ["ident"], writes)

        TOP = (ARENA_BYTES - (8 * 4096 * 2 + 8 * 8 * 2 + 64 + 8 * 8 * 4 + 4 * 4096)) // 64 * 64
        AR.off = TOP
        wkv = AR.alloc([8, 4096], BF16)
        wf = AR.alloc([8, 8], BF16)
        wfst = AR.alloc([8, 8], F32)
        wst_top = [AR.alloc([1024], F32) for _ in range(4)]
        AR.reset()
        c_sb = AR.alloc([8], F32)
        c_sig = AR.alloc([8], F32)
        c_act = AR.alloc([8], F32)
        bada_r = AR.alloc([3 * D], F32)
        gn_r = AR.alloc([D], F32)
        mod_r = AR.alloc([3 * D], F32)
        gs_r = AR.alloc([D], F32)
        wada = [AR.alloc([3 * D], F32) for _ in range(2)]
        posi = AR.alloc([NT_ALL], I32)
        posf = AR.alloc([NT_ALL], F32)
        posoi = AR.alloc([NT_OWN], I32)
        posof = AR.alloc([NT_OWN], F32)
        ang = AR.alloc([32, 32], F32)
        rk = AR.alloc([32, 32], F32)
        rki = AR.alloc([32, 32], I32)
        rr = AR.alloc([32, 32], F32)
        rm = AR.alloc([32, 32], F32)
        phase0_end = AR.off

        def load_bc(dst, src, n, name):
            P.dma(SP, dst, src.partition_broadcast(128), writes=[name], key=R("ld"))

        P.dma(SP, c_sb, cT_d, writes=["c_sb"], key=R("ld"))
        P.dma(SP, bada_r[0:1, :], b_ada, writes=["bada_r"], key=R("ld"))
        P.dma(SP, gn_r[0:1, :], g_norm, writes=["gn_r"], key=R("ld"))
        load_bc(gfinB[:], g_final, D, "gfinB")
        load_bc(gsubB[:], g_subln, 128, "gsubB")
        load_bc(bfB[:], b_forget, 8, "bfB")
        for n_, src in enumerate((lq1, lk1, lq2, lk2)):
            load_bc(lv[:, n_, :], src, 64, f"lv{n_}")
        load_bc(invf[:], invf_d, 64, "invf")
        load_bc(jsel[:], jsel_d, 4, "jsel")
        P.dma(SP, maskA[:].rearrange("p a b -> p (a b)"), maskA_d, writes=["maskA"], key=R("ld"))
        P.dma(SP, maskB[:].rearrange("p a b -> p (a b)"), maskB_d, writes=["maskB"], key=R("ld"))
        P.dma(SP, posi, pos_all_d, writes=["Aposi"], key=R("ld"))
        P.dma(SP, posoi, pos_own_d, writes=["Oposi"], key=R("ld"))

        P.op(POOL, lambda e: e.memset(identf[:], 0.0), writes=["identf"])
        P.op(POOL, lambda e: e.affine_select(out=identf[:], in_=identf[:], pattern=[[-1, 128]], compare_op=ALU.not_equal, fill=1.0, base=0, channel_multiplier=1), reads=["identf"], writes=["identf"])
        P.op(POOL, lambda e: e.memset(trif[:], 1.0), writes=["trif"])
        P.op(POOL, lambda e: e.affine_select(out=trif[:], in_=trif[:], pattern=[[1, 128]], compare_op=ALU.is_ge, fill=0.0, base=0, channel_multiplier=-1), reads=["trif"], writes=["trif"])
        P.op(POOL, lambda e: e.memset(onesf[:], 1.0), writes=["onesf"])
        P.op(POOL, lambda e: e.memset(epsT[:, 0:1], 1e-6), writes=["epsT"])
        P.op(POOL, lambda e: e.memset(epsT[:, 1:2], 1e-5), writes=["epsT"])
        cp(DVE, ident[:], identf[:], ["identf"], ["ident"])

        act(c_sig, c_sb, AF.Sigmoid, ["c_sb"], ["c_sig"])
        tt(DVE, c_act, c_sb, c_sig, ALU.mult, ["c_sb", "c_sig"], ["c_act"])
        for ch in range(8):
            sl = ch % 2
            P.dma(SP, wada[sl], w_ada[ch * 128:(ch + 1) * 128, :], writes=[f"wada{sl}"], key=f"wada{sl}")
            for g in range(6):
                mm(bk[g][0:1, :], c_act[:, ch:ch + 1], wada[sl][:, g * 512:(g + 1) * 512], ch == 0, ch == 7,
                   ["c_act", f"wada{sl}"], [BK[g]])
        for g in range(6):
            tt(DVE, mod_r[0:1, g * 512:(g + 1) * 512], bk[g][0:1, :], bada_r[0:1, g * 512:(g + 1) * 512], ALU.add,
               [BK[g], "bada_r"], [f"mod{g}"])
        for hf in range(2):
            stt(DVE, gs_r[0:1, hf * 512:(hf + 1) * 512], mod_r[0:1, D + hf * 512:D + (hf + 1) * 512], 1.0,
                gn_r[0:1, hf * 512:(hf + 1) * 512], ALU.add, ALU.mult, [f"mod{2 + hf}", "gn_r"], [f"gs_r{hf}"])
        bi = 0
        for (row, off, dst, rn) in ((gs_r, 0, gsB, "gs_r"), (mod_r, 0, shiftB, "mod"), (mod_r, 2 * D, gateB, "mod")):
            for hf in range(2):
                b = 6 + (bi % 2)
                bi += 1
                rname = f"gs_r{hf}" if rn == "gs_r" else f"mod{(off // 512) + hf}"
                mm(bk[b][:, :], onesf[0:1, :], row[0:1, off + hf * 512:off + (hf + 1) * 512], True, True,
                   ["onesf", rname], [BK[b]])
                cp(ACT if hf else DVE, dst[:, hf * 512:(hf + 1) * 512], bk[b][:, :], [BK[b]], [dst.tensor.name + str(hf)] if False else [f"{id(dst)}_{hf}"])
        RES_GS = [f"{id(gsB)}_0", f"{id(gsB)}_1"]
        RES_SH = [f"{id(shiftB)}_0", f"{id(shiftB)}_1"]
        RES_GT = [f"{id(gateB)}_0", f"{id(gateB)}_1"]
        for n_ in range(2):
            stt(DVE, lv[:, 2 * n_, :], lv[:, 2 * n_, :], 1.0, lv[:, 2 * n_ + 1, :], ALU.mult, ALU.mult,
                [f"lv{2 * n_}", f"lv{2 * n_ + 1}"], [f"lv{2 * n_}", f"lamt{2 + n_}"], accum_out=lamt[:, 2 + n_:3 + n_])
            act(lamt[:, 4 + n_:5 + n_], lamt[:, 2 + n_:3 + n_], AF.Exp, [f"lamt{2 + n_}"], [f"lamt{4 + n_}"])
        tt(DVE, lamt[:, 0:1], lamt[:, 4:5], lamt[:, 5:6], ALU.subtract, ["lamt4", "lamt5"], ["lam"])
        ts(DVE, lamt[:, 0:1], lamt[:, 0:1], 0.2, None, ALU.add, None, ["lam"], ["lam"])
        ts(DVE, lamt[:, 1:2], lamt[:, 0:1], -1.0, None, ALU.mult, None, ["lam"], ["neglam"])

        wstn_ = 0
        for (col0_, dst0_) in ((1024, 0), (5120, 2048)):
            for ch_ in range(8):
                for c0_ in (0, 1024):
                    sl_ = wstn_ % 4
                    ceng_ = POOL if wstn_ % 4 == 3 else DVE
                    wstn_ += 1
                    P.dma(SP, wst_top[sl_], w_in[ch_ * 128:(ch_ + 1) * 128, col0_ + c0_:col0_ + c0_ + 1024],
                          writes=[f"wstT{sl_}"], key=f"wstT{sl_}")
                    P.op(ceng_, lambda e, o_=wkv[:, ch_, dst0_ + c0_:dst0_ + c0_ + 1024], i_=wst_top[sl_]: e.tensor_copy(out=o_, in_=i_),
                         [f"wstT{sl_}"], ["wkv"])
        P.dma(SP, wfst, w_in[:, 8192:8200].rearrange("(c p) n -> p c n", p=128), writes=["wfst"], key="wfst")
        P.op(POOL, lambda e: e.tensor_copy(out=wf, in_=wfst), ["wfst"], ["wf"])

        def rope_tables(pos_i, pos_f, nt, cos_dst, sin_dst, tag, posres):
            cp(DVE, pos_f, pos_i, [posres], [tag + "posf"])
            a_ = ang[:, 0:nt, :]
            k_ = rk[:, 0:nt, :]
            ki_ = rki[:, 0:nt, :]
            r_ = rr[:, 0:nt, :]
            m_ = rm[:, 0:nt, :]
            tt(DVE, a_, pos_f.unsqueeze(2).to_broadcast([128, nt, 32]), invf[:, 0:32].unsqueeze(1).to_broadcast([128, nt, 32]),
               ALU.mult, [tag + "posf", "invf"], ["ang"])
            tt(DVE, m_, pos_f.unsqueeze(2).to_broadcast([128, nt, 32]), invf[:, 32:64].unsqueeze(1).to_broadcast([128, nt, 32]),
               ALU.mult, [tag + "posf", "invf"], ["rm"])
            tt(DVE, a_, a_, m_, ALU.add, ["ang", "rm"], ["ang"])
            for which, dst in ((0, sin_dst), (1, cos_dst)):
                src = a_
                if which == 1:
                    ts(DVE, r_, a_, math.pi / 2, None, ALU.add, None, ["ang"], ["rr"])
                    src = r_
                ts(DVE, k_, src, 1.0 / TWO_PI, None, ALU.mult, None, ["ang", "rr"], ["rk"])
                cp(DVE, ki_, k_, ["rk"], ["rki"])
                cp(DVE, k_, ki_, ["rki"], ["rk"])
                stt(DVE, m_, k_, -C1, src, ALU.mult, ALU.add, ["rk", "ang", "rr"], ["rm"])
                stt(DVE, r_, k_, -C2, m_, ALU.mult, ALU.add, ["rk", "rm"], ["rr"])
                ts(DVE, m_, r_, math.pi, -TWO_PI, ALU.is_gt, ALU.mult, ["rr"], ["rm"])
                tt(DVE, r_, r_, m_, ALU.add, ["rr", "rm"], ["rr"])
                ts(DVE, m_, r_, -math.pi, TWO_PI, ALU.is_lt, ALU.mult, ["rr"], ["rm"])
                tt(DVE, r_, r_, m_, ALU.add, ["rr", "rm"], ["rr"])
                ts(DVE, r_, r_, math.pi, -math.pi, ALU.min, ALU.max, ["rr"], ["rr"])
                act(dst, r_, AF.Sin, ["rr"], [tag + ("cos" if which else "sin")])

        AR.off = phase0_end
        cos_all = AR.alloc([NT_ALL, 32], F32)
        sin_all = AR.alloc([NT_ALL, 32], F32)
        for hfi in range(2):
            sl32 = slice(hfi * 32, (hfi + 1) * 32)
            rope_tables(posi[:, sl32], posf[:, sl32], 32, cos_all[:, sl32, :], sin_all[:, sl32, :], "A", "Aposi")
        rope_tables(posoi, posof, NT_OWN, cos_own[:], sin_own[:], "O", "Oposi")
        assert AR.off <= TOP, (AR.off, TOP)
        P.barrier(POOL, lambda e: e.memset(stat[:, 0:1], 0.0))
        AR.reset()
        cos_all2 = AR.alloc([NT_ALL, 32], F32)
        sin_all2 = AR.alloc([NT_ALL, 32], F32)
        cp(DVE, cos_all2, cos_all, [], ["cosA"])
        cp(DVE, sin_all2, sin_all, [], ["sinA"])
        P.barrier(POOL, lambda e: e.memset(stat[:, 0:1], 0.0))
        cos_all, sin_all = cos_all2, sin_all2
        wst = None
        xt = [AR.alloc([D], F32) for _ in range(3)]
        junk = AR.alloc([D], BF16)
        hn = AR.alloc([D], F32)
        hb = [AR.alloc([D], BF16) for _ in range(3)]
        hT = [AR.alloc([8, 512], BF16) for _ in range(2)]
        ra = AR.alloc([512], F32)
        rb = AR.alloc([512], F32)
        kar = AR.alloc([D], BF16)
        kTa_st = AR.alloc([8, 512], BF16)
        kTb_st = AR.alloc([8, 512], BF16)
        v_st = [AR.alloc([16, 129], BF16) for _ in range(2)]
        ylog = AR.alloc([NT_ALL, 8], F32)
        lsb = AR.alloc([NT_ALL, 8], F32)
        cumt = AR.alloc([NT_ALL, 8], F32)
        totb = AR.alloc([NT_ALL, 8], F32)
        offs = AR.alloc([NT_ALL, 8], F32)
        assert AR.off <= TOP, (AR.off, TOP)

        wst_n = [0]

        def load_weight_cols(dst_fn, src, col0, ncols, reads_tag):
            for ch in range(8):
                for c0 in range(0, ncols, 1024):
                    n = min(1024, ncols - c0)
                    sl = wst_n[0] % len(wst)
                    ceng = (DVE, ACT, POOL, DVE, ACT)[wst_n[0] % 5]
                    wst_n[0] += 1
                    P.dma(SP, wst[sl][:, 0:n], src[ch * 128:(ch + 1) * 128, col0 + c0:col0 + c0 + n],
                          writes=[f"wst{sl}"], key=f"wst{sl}")
                    cp(ceng, dst_fn(ch, c0, n), wst[sl][:, 0:n], [f"wst{sl}"], [reads_tag])

        for s_ in range(2):
            P.op(POOL, lambda e, s_=s_: e.memset(v_st[s_][:, :, 128:129], 1.0), writes=[f"v_st{s_}"])

        gen_rot = [0]

        def gen_bank():
            b = 5 + (gen_rot[0] % 3)
            gen_rot[0] += 1
            return b

        def norm_tile(x_src_ap, xs, hs, statcol, shift_add=True):
            P.dma(SP, xt[xs], x_src_ap, writes=[f"xt{xs}"], key=f"xt{xs}")
            sc = stat[:, statcol:statcol + 1]
            act(junk, xt[xs], AF.Square, [f"xt{xs}"], ["junk", f"statc{xs}"], accum_out=sc)
            act(sc, sc, AF.Sqrt, [f"statc{xs}"], [f"statc{xs}"], scale=1.0 / D, bias=epsT[:, 0:1])
            P.op(DVE, lambda e: e.reciprocal(out=sc, in_=sc), [f"statc{xs}"], [f"statc{xs}"])
            stt(DVE, hn, xt[xs], sc, gsB[:], ALU.mult, ALU.mult, [f"xt{xs}", f"statc{xs}"] + RES_GS, ["hn"])
            tt(DVE, hb[hs], hn, shiftB[:], ALU.add, ["hn"] + RES_SH, [f"hb{hs}"])

        def transpose_to(dst3, src2, nchunks, src_res, dst_res, bank):
            for c in range(nchunks):
                tr(bkb[bank][:, c * 128:(c + 1) * 128], src2[:, c * 128:(c + 1) * 128], [src_res], [BK[bank]])
            act(dst3, bkb[bank][:, 0:nchunks * 128].rearrange("p (c t) -> p c t", t=128), AF.Copy, [BK[bank]], [dst_res])

        def rope_bank(bank, cos_t, sin_t, dst2, dst_res):
            v = bk[bank][:, :].rearrange("p (b h j) -> p b h j", h=2, j=32)
            t1 = v[:, :, 0, :]
            t2 = v[:, :, 1, :]
            cb = cos_t.unsqueeze(1).to_broadcast([128, 8, 32])
            sbb = sin_t.unsqueeze(1).to_broadcast([128, 8, 32])
            ra3 = ra[:, 0:256].rearrange("p (b j) -> p b j", j=32)
            rb3 = rb[:, 0:256].rearrange("p (b j) -> p b j", j=32)
            d4 = dst2.rearrange("p (b h j) -> p b h j", h=2, j=32)
            tt(DVE, ra3, t1, cb, ALU.mult, [BK[bank], "cs"], ["ra"])
            tt(DVE, rb3, t2, sbb, ALU.mult, [BK[bank], "cs"], ["rb"])
            tt(DVE, d4[:, :, 0, :], ra3, rb3, ALU.subtract, ["ra", "rb"], [dst_res])
            tt(DVE, ra3, t1, sbb, ALU.mult, [BK[bank], "cs"], ["ra"])
            tt(DVE, rb3, t2, cb, ALU.mult, [BK[bank], "cs"], ["rb"])
            tt(DVE, d4[:, :, 1, :], ra3, rb3, ALU.add, ["ra", "rb"], [dst_res])

        def stageN_all(t):
            norm_tile(x_all[t * 128:(t + 1) * 128, :], t % 3, t % 3, t)

        def stageA_all(t):
            g, t4 = t // 4, t % 4
            gs_ = g % 2
            transpose_to(hT[gs_][:, :, t4 * 128:(t4 + 1) * 128], hb[t % 3], 8, f"hb{t % 3}", f"hT{gs_}_{t4}", 0)

        def stageB_all(t):
            g, t4 = t // 4, t % 4
            gs_ = g % 2
            lhs = [hT[gs_][:, c, t4 * 128:(t4 + 1) * 128] for c in range(8)]
            for hf in range(2):
                b = 2 + hf
                for c in range(8):
                    mm(bk[b][:, :], lhs[c], wkv[:, c, hf * 512:(hf + 1) * 512], c == 0, c == 7, [f"hT{gs_}_{t4}", "wkv"], [BK[b]])
                rope_bank(b, cos_all[:, t, :], sin_all[:, t, :], kar[:, hf * 512:(hf + 1) * 512], "kar")
            for c in range(8):
                mm(bk[1][:, 0:8], lhs[c], wf[:, c, :], c == 0, c == 7, [f"hT{gs_}_{t4}", "wf"], [BK[1]])
            tt(DVE, ylog[:, t, :], bk[1][:, 0:8], bfB[:], ALU.add, [BK[1], "bfB"], ["ylog"])
            vs_ = t % 2
            for br, col0 in ((0, 1024), (1, 3072)):
                for hf in range(2):
                    b = gen_bank()
                    for c in range(8):
                        mm(bk[b][:, :], lhs[c], wkv[:, c, col0 + hf * 512:col0 + (hf + 1) * 512], c == 0, c == 7,
                           [f"hT{gs_}_{t4}", "wkv"], [BK[b]])
                    h0 = br * 8 + hf * 4
                    act(v_st[vs_][:, h0:h0 + 4, 0:128], bk[b][:, :].rearrange("p (h c) -> p h c", c=128), AF.Copy,
                        [BK[b]], [f"v_st{vs_}"])
            P.dma(POOL, v_s[:, :, t, :].rearrange("h p c -> p h c"), v_st[vs_], reads=[f"v_st{vs_}"], writes=["v_s"], key=f"v_st{vs_}")

        def stageC_all(t):
            g, t4 = t // 4, t % 4
            gs_ = g % 2
            transpose_to(kTa_st[:, :, t4 * 128:(t4 + 1) * 128], kar, 8, "kar", "kTa_st", 4)
            if t4 != 3:
                return
            for h in range(8):
                b = gen_bank()
                for c in range(8):
                    mm(bk[b][:, :], wkv[:, c, 2048 + h * 128:2048 + (h + 1) * 128], hT[gs_][:, c, :], c == 0, c == 7,
                       [f"hT{gs_}_0", f"hT{gs_}_1", f"hT{gs_}_2", f"hT{gs_}_3", "wkv"], [BK[b]])
                act(kTb_st[:, h, :], bk[b][:, :], AF.Copy, [BK[b]], ["kTb_st"])
            P.dma(POOL, kT_s[0:8, :, g * 512:(g + 1) * 512].rearrange("h p t -> p h t"), kTa_st, reads=["kTa_st"], writes=["kT_s"], key="kTa_st")
            P.dma(POOL, kT_s[8:16, :, g * 512:(g + 1) * 512].rearrange("h p t -> p h t"), kTb_st, reads=["kTb_st"], writes=["kT_s"], key="kTb_st")

        stageN_all(0)
        stageN_all(1)
        stageN_all(2)
        stageA_all(0)
        stageA_all(1)
        for t in range(NT_ALL):
            if t + 3 < NT_ALL:
                stageN_all(t + 3)
            if t + 2 < NT_ALL:
                stageA_all(t + 2)
            stageB_all(t)
            stageC_all(t)

        yl2 = ylog.rearrange("p a b -> p (a b)")
        ls2 = lsb.rearrange("p a b -> p (a b)")
        act(ls2, yl2, AF.Exp, ["ylog"], ["lsb"], scale=-1.0)
        ts(DVE, ls2, ls2, 1.0, None, ALU.add, None, ["lsb"], ["lsb"])
        act(ls2, ls2, AF.Ln, ["lsb"], ["lsb"])
        mm(bk[2][:, :], trif[:], ls2, True, True, ["trif", "lsb"], [BK[2]])
        mm(bk[3][:, :], onesf[:], ls2, True, True, ["onesf", "lsb"], [BK[3]])
        cp(DVE, cumt.rearrange("p a b -> p (a b)"), bk[2][:, :], [BK[2]], ["cumt"])
        cp(DVE, totb.rearrange("p a b -> p (a b)"), bk[3][:, :], [BK[3]], ["totb"])
        P.op(DVE, lambda e: e.memset(offs[:, 0, :], 0.0), writes=["offs"])
        for t in range(1, NT_ALL):
            tt(DVE, offs[:, t, :], offs[:, t - 1, :], totb[:, t - 1, :], ALU.add, ["offs", "totb"], ["offs"])
        tt(DVE, Lall[:], cumt, offs, ALU.add, ["cumt", "offs"], ["Lall"])
        L4 = Lall[:].rearrange("p (i d) h -> p i d h", d=4)
        ts(DVE, Lown[:], L4[:, :, 0, :], jsel[:, 0:1], None, ALU.mult, None, ["Lall", "jsel"], ["Lown"])
        for d_ in range(1, 4):
            stt(DVE, Lown[:], L4[:, :, d_, :], jsel[:, d_:d_ + 1], Lown[:], ALU.mult, ALU.add, ["Lall", "jsel", "Lown"], ["Lown"])
        mm(bk[4][:, 0:128], Lown[:].rearrange("p a b -> p (a b)"), identf[:], True, True, ["Lown", "identf"], [BK[4]])
        ts(DVE, ra[:, 0:128], bk[4][:, 0:128], -1.0, None, ALU.mult, None, [BK[4]], ["ra"])
        for i in range(NT_OWN):
            P.dma(POOL, negL_s[:, i * 128:(i + 1) * 128], ra[i * 8:(i + 1) * 8, 0:128], reads=["ra"], writes=["negL_s"], key=R("st"))
        if dbg:
            P.dma(POOL, dbgL[:, 0:512], Lall[:].rearrange("p a b -> p (a b)"), reads=["Lall"], key=R("st"))
            P.dma(POOL, dbgL[:, 512:640], Lown[:].rearrange("p a b -> p (a b)"), reads=["Lown"], key=R("st"))
            P.dma(POOL, dbgL[:, 640:642], lamt[:, 0:2], reads=["lam", "neglam"], key=R("st"))

        P.barrier(POOL, lambda e: e.memset(stat[:, 0:1], 0.0))
        AR.reset()
        wown = AR.alloc([8, 3072], BF16)
        wst = [AR.alloc([1024], F32) for _ in range(8)]
        xt = [AR.alloc([D], F32) for _ in range(3)]
        junk = AR.alloc([D], BF16)
        hn = AR.alloc([D], F32)
        hb = [AR.alloc([D], BF16) for _ in range(3)]
        hTo = AR.alloc([8, NT_OWN * 128], BF16)
        ra = AR.alloc([512], F32)
        rb = AR.alloc([512], F32)
        qar = AR.alloc([D], BF16)
        qTa_st = AR.alloc([8, 512], BF16)
        qTb_st = AR.alloc([8, 512], BF16)
        sgt = [AR.alloc([512], F32) for _ in range(2)]
        o_st = [AR.alloc([2048], BF16) for _ in range(2)]

        load_weight_cols(lambda ch, c0, n: wown[:, ch, c0:c0 + n], w_in, 0, 1024, "wown")
        load_weight_cols(lambda ch, c0, n: wown[:, ch, 1024 + c0:1024 + c0 + n], w_in, 4096, 1024, "wown")
        load_weight_cols(lambda ch, c0, n: wown[:, ch, 2048 + c0:2048 + c0 + n], w_in, 3072, 1024, "wown")
        osn = [0]

        def silu_out(col0, i, head0):
            os_ = osn[0] % 2
            osn[0] += 1
            for hf in range(2):
                b = gen_bank()
                for c in range(8):
                    mm(bk[b][:, :], hTo[:, c, i * 128:(i + 1) * 128], wown[:, c, col0 + hf * 512:col0 + (hf + 1) * 512],
                       c == 0, c == 7, [f"hTo{i}", "wown"], [BK[b]])
                act(sgt[hf], bk[b][:, :], AF.Sigmoid, [BK[b]], [f"sgt{hf}"])
                tt(DVE, o_st[os_][:, hf * 512:(hf + 1) * 512], bk[b][:, :], sgt[hf], ALU.mult, [BK[b], f"sgt{hf}"], [f"o_st{os_}"])
            P.dma(POOL, sz_s[head0:head0 + 8, :, i, :].rearrange("h p c -> p h c"),
                  o_st[os_][:, 0:1024].rearrange("p (h c) -> p h c", c=128), reads=[f"o_st{os_}"], writes=["sz_s"], key=f"o_st{os_}")

        def stageN_own(i):
            norm_tile(x_own[i * 128:(i + 1) * 128, :], i % 3, i % 3, 64 + i)

        def stageA_own(i):
            transpose_to(hTo[:, :, i * 128:(i + 1) * 128], hb[i % 3], 8, f"hb{i % 3}", f"hTo{i}", 0)

        def stageB_own(i):
            lhs = [hTo[:, c, i * 128:(i + 1) * 128] for c in range(8)]
            for hf in range(2):
                b = 2 + hf
                for c in range(8):
                    mm(bk[b][:, :], lhs[c], wown[:, c, hf * 512:(hf + 1) * 512], c == 0, c == 7, [f"hTo{i}", "wown"], [BK[b]])
                rope_bank(b, cos_own[:, i, :], sin_own[:, i, :], qar[:, hf * 512:(hf + 1) * 512], "qar")
            silu_out(2048, i, 0)

        def stageC_own(i):
            g, t4 = i // 4, i % 4
            transpose_to(qTa_st[:, :, t4 * 128:(t4 + 1) * 128], qar, 8, "qar", "qTa_st", 4)
            if t4 != 3:
                return
            for h in range(8):
                b = gen_bank()
                for c in range(8):
                    mm(bk[b][:, :], wown[:, c, 1024 + h * 128:1024 + (h + 1) * 128], hTo[:, c, g * 512:(g + 1) * 512], c == 0, c == 7,
                       [f"hTo{4 * g}", f"hTo{4 * g + 1}", f"hTo{4 * g + 2}", f"hTo{4 * g + 3}", "wown"], [BK[b]])
                act(qTb_st[:, h, :], bk[b][:, :], AF.Copy, [BK[b]], ["qTb_st"])
            P.dma(POOL, qT_s[0:8, :, g * 512:(g + 1) * 512].rearrange("h p t -> p h t"), qTa_st, reads=["qTa_st"], writes=["qT_s"], key="qTa_st")
            P.dma(POOL, qT_s[8:16, :, g * 512:(g + 1) * 512].rearrange("h p t -> p h t"), qTb_st, reads=["qTb_st"], writes=["qT_s"], key="qTb_st")

        stageN_own(0)
        stageN_own(1)
        stageN_own(2)
        stageA_own(0)
        stageA_own(1)
        for i in range(NT_OWN):
            if i + 3 < NT_OWN:
                stageN_own(i + 3)
            if i + 2 < NT_OWN:
                stageA_own(i + 2)
            stageB_own(i)
            stageC_own(i)

        load_weight_cols(lambda ch, c0, n: wown[:, ch, c0:c0 + n], w_in, 7168, 1024, "wown")
        load_weight_cols(lambda ch, c0, n: wown[:, ch, 1024 + c0:1024 + c0 + n], w_in, 8200, 2048, "wown")
        for i in range(NT_OWN):
            silu_out(0, i, 8)
            os_ = osn[0] % 2
            osn[0] += 1
            for q4 in range(4):
                b = gen_bank()
                for c in range(8):
                    mm(bk[b][:, :], hTo[:, c, i * 128:(i + 1) * 128], wown[:, c, 1024 + q4 * 512:1024 + (q4 + 1) * 512],
                       c == 0, c == 7, [f"hTo{i}", "wown"], [BK[b]])
                act(o_st[os_][:, q4 * 512:(q4 + 1) * 512], bk[b][:, :], AF.Sigmoid, [BK[b]], [f"o_st{os_}"])
            P.dma(POOL, sm_s[i], o_st[os_], reads=[f"o_st{os_}"], writes=["sm_s"], key=f"o_st{os_}")

        P.barrier(POOL, lambda e: e.memset(stat[:, 0:1], 0.0))
        AR.reset()
        Kb = [AR.alloc([S_ALL], BF16) for _ in range(2)]
        Vb = [AR.alloc([NT_ALL, 129], BF16) for _ in range(2)]
        Qb = [AR.alloc([NT_OWN * 128], BF16) for _ in range(2)]
        SZb = [AR.alloc([NT_OWN, 128], BF16) for _ in range(2)]
        NPB = 7
        LAG = 5
        Pb = [AR.alloc([2, 512], BF16) for _ in range(NPB)]
        tmpb = [AR.alloc([512], F32) for _ in range(5)]
        FqB = [AR.alloc([NT_OWN * 128], F32) for _ in range(2)]
        dg = [AR.alloc([128], F32) for _ in range(2)]
        usth = [AR.alloc([NT_OWN, 128], BF16) for _ in range(2)]
        obuf2 = [AR.alloc([NT_OWN, 128], F32) for _ in range(2)]
        accS = AR.alloc([8, 129], F32)
        eo1 = AR.alloc([128], F32)
        ejk = AR.alloc([128], F32)
        est = AR.alloc([16], F32)
        mscol2 = [AR.alloc([NT_OWN], F32) for _ in range(2)]

        SCALE_A = 64 ** -0.5
        SCALE_B = 128 ** -0.5

        def build_fq(h8):
            fs = h8 % 2
            P.dma(SP, FqB[fs], negL_s[h8:h8 + 1, :].partition_broadcast(128), reads=["negL_s"], writes=[f"FqB{fs}"], key=f"FqB{fs}")

        steps = []
        ccn = 0
        for hh in range(16):
            for c in range(4):
                nk = 16 * c + 16
                for k in range(nk):
                    steps.append((hh, c, k, ccn, k == 0, k == nk - 1))
                ccn += 1

        def head_loads(hh):
            hs = hh % 2
            P.dma(SP, Kb[hs], kT_s[hh], reads=["kT_s"], writes=[f"K{hs}"], key=f"K{hs}")
            P.dma(SP, Vb[hs], v_s[hh], reads=["v_s"], writes=[f"V{hs}"], key=f"V{hs}")
            P.dma(SP, Qb[hs], qT_s[hh], reads=["qT_s"], writes=[f"Q{hs}"], key=f"Q{hs}")
            P.dma(SP, SZb[hs], sz_s[hh], reads=["sz_s"], writes=[f"SZ{hs}"], key=f"SZ{hs}")
            if hh == 7:
                build_fq(0)
            elif 8 <= hh < 15:
                build_fq(hh - 8 + 1)

        def amin_of(c, k):
            return max(0, -(-(k - 16 * c - 3) // 4))

        def front(n):
            hh, c, k, cc, first, last = steps[n]
            is_a = hh < 8
            hs = hh % 2
            h8 = hh % 8
            a_min = amin_of(c, k)
            col0 = a_min * 128
            ps = n % NPB
            if is_a:
                sp = n % 2
                s3 = pairs[sp][:, :].rearrange("p (m q) -> p m q", m=2)
                for m in range(2):
                    pr = slice(m * 64, (m + 1) * 64)
                    mm(bk[2 * sp + m][:, col0:512], Kb[hs][pr, k * 128:(k + 1) * 128],
                       Qb[hs][pr, c * 512 + col0:(c + 1) * 512], True, True, [f"K{hs}", f"Q{hs}"], [BK[2 * sp + m]])
                act(Pb[ps][:, :, col0:512], s3[:, :, col0:512], AF.Exp, [BK[2 * sp], BK[2 * sp + 1]], [f"P{ps}"], scale=SCALE_A)
                for a in range(a_min, 4):
                    d_ = k - 16 * c - 4 * a
                    if 0 <= d_ <= 3:
                        blk = slice(a * 128, (a + 1) * 128)
                        tt(POOL, Pb[ps][:, :, blk], Pb[ps][:, :, blk], maskA[:, d_, :].unsqueeze(1).to_broadcast([128, 2, 128]),
                           ALU.mult, [f"P{ps}", "maskA"], [f"P{ps}"])
            else:
                sbk = n % 4
                ts_ = n % 5
                fs = h8 % 2
                mm(bk[sbk][:, col0:512], Kb[hs][:, k * 128:(k + 1) * 128], Qb[hs][:, c * 512 + col0:(c + 1) * 512], True, True,
                   [f"K{hs}", f"Q{hs}"], [BK[sbk]])
                stt(DVE, tmpb[ts_][:, col0:512], bk[sbk][:, col0:512], SCALE_B, FqB[fs][:, c * 512 + col0:(c + 1) * 512],
                    ALU.mult, ALU.add, [BK[sbk], f"FqB{fs}"], [f"tmp{ts_}"])
                for a in range(a_min, 4):
                    d_ = k - 16 * c - 4 * a
                    if 0 <= d_ <= 3:
                        blk = slice(a * 128, (a + 1) * 128)
                        tt(POOL, tmpb[ts_][:, blk], tmpb[ts_][:, blk], maskB[:, d_, :], ALU.add, [f"tmp{ts_}", "maskB"], [f"tmp{ts_}"])
                act(Pb[ps][:, 0, col0:512], tmpb[ts_][:, col0:512], AF.Exp, [f"tmp{ts_}", "Lall"], [f"P{ps}"],
                    bias=Lall[:, k, h8:h8 + 1], scale=1.0)

        def acc_a(m, a):
            b = 4 + 2 * m + a // 2
            o_ = (a % 2) * 129
            return b, bk[b][:, o_:o_ + 129]

        def acc_b(cc, a):
            b = 4 + 2 * (cc % 2) + a // 2
            o_ = (a % 2) * 129
            return b, bk[b][:, o_:o_ + 129]

        def zero_banks(banks):
            for b in banks:
                P.op(DVE, lambda e, b=b: e.memset(bk[b][:, 0:258], 0.0), writes=[BK[b]])

        def next_is_a(n):
            return n + 1 < NS and steps[n + 1][0] < 8

        def back(n):
            hh, c, k, cc, first, last = steps[n]
            is_a = hh < 8
            hs = hh % 2
            a_min = amin_of(c, k)
            ps = n % NPB
            if n == 0:
                zero_banks((4, 5, 6, 7))
            if is_a:
                obuf, mscol = obuf2[hs], mscol2[hs]
                OB, MS = f"obuf{hs}", f"mscol{hs}"
                for a in range(a_min, 4):
                    blk = slice(a * 128, (a + 1) * 128)
                    for m in range(2):
                        b, acc = acc_a(m, a)
                        mm(acc, Pb[ps][:, m, blk], Vb[hs][:, k, :], False, False, [f"P{ps}", f"V{hs}"], [BK[b]], skip=True)
                if not last:
                    return
                for j, b in enumerate((4, 5, 6, 7)):
                    src = bk[b][:, 0:258].rearrange("p (a c) -> p a c", c=129)
                    cp(DVE, accS[:, 2 * j:2 * j + 2, :], src, [BK[b]], [f"accS{j}"])
                if n + 1 < NS:
                    zero_banks((4, 5, 6, 7))
                for a in range(4):
                    i = 4 * c + a
                    j1, j2 = a // 2, 2 + a // 2
                    acc1 = accS[:, 2 * j1 + a % 2, :]
                    acc2 = accS[:, 2 * j2 + a % 2, :]
                    P.op(DVE, lambda e, acc1=acc1: e.reciprocal(out=est[:, 0:1], in_=acc1[:, 128:129]), [f"accS{j1}"], ["est0"])
                    P.op(DVE, lambda e, acc2=acc2: e.reciprocal(out=est[:, 1:2], in_=acc2[:, 128:129]), [f"accS{j2}"], ["est1"])
                    tt(DVE, est[:, 1:2], est[:, 1:2], lamt[:, 1:2], ALU.mult, ["est1", "neglam"], ["est1"])
                    ts(DVE, eo1, acc1[:, 0:128], est[:, 0:1], None, ALU.mult, None, [f"accS{j1}", "est0"], ["eo1"])
                    stt(DVE, obuf[:, i, :], acc2[:, 0:128], est[:, 1:2], eo1, ALU.mult, ALU.add, [f"accS{j2}", "est1", "eo1"], [OB])
                    stt(DVE, ejk, obuf[:, i, :], 1.0, obuf[:, i, :], ALU.mult, ALU.mult, [OB], ["ejk", MS], accum_out=mscol[:, i:i + 1])
                if c != 3:
                    return

                def head_epi(hh=hh, hs=hs, obuf=obuf, mscol=mscol, OB=OB, MS=MS):
                    act(mscol, mscol, AF.Sqrt, [MS], [MS], scale=1.0 / 128, bias=epsT[:, 1:2])
                    P.op(DVE, lambda e: e.reciprocal(out=mscol, in_=mscol), [MS], [MS])
                    ts(DVE, mscol, mscol, 0.8, None, ALU.mult, None, [MS], [MS])
                    tt(DVE, obuf, obuf, mscol.unsqueeze(2).to_broadcast([128, NT_OWN, 128]), ALU.mult, [OB, MS], [OB])
                    tt(DVE, obuf, obuf, gsubB[:].unsqueeze(1).to_broadcast([128, NT_OWN, 128]), ALU.mult, [OB, "gsubB"], [OB])
                    tt(DVE, usth[hs], obuf, SZb[hs], ALU.mult, [OB, f"SZ{hs}"], [f"usth{hs}"])
                    P.dma(POOL, u_s[:, :, hh * 128:(hh + 1) * 128].rearrange("i p c -> p i c"), usth[hs],
                          reads=[f"usth{hs}"], writes=["u_s"], key=f"usth{hs}")
                deferred.append((n + 28, head_epi))
                return
            for a in range(a_min, 4):
                blk = slice(a * 128, (a + 1) * 128)
                b, acc = acc_b(cc, a)
                mm(acc, Pb[ps][:, 0, blk], Vb[hs][:, k, :], False, False, [f"P{ps}", f"V{hs}"], [BK[b]], skip=True)
            if not last:
                return
            if n + 1 < NS:
                sn_ = (cc + 1) % 2
                zero_banks((4 + 2 * sn_, 5 + 2 * sn_))
            for a in range(4):
                i = 4 * c + a
                b1, acc1 = acc_b(cc, a)
                P.op(DVE, lambda e, acc1=acc1: e.reciprocal(out=est[:, 0:1], in_=acc1[:, 128:129]), [BK[b1]], ["est0"])
                stt(DVE, usth[hs][:, i, :], acc1[:, 0:128], est[:, 0:1], SZb[hs][:, i, :], ALU.mult, ALU.mult,
                    [BK[b1], "est0", f"SZ{hs}"], [f"usth{hs}"])
            if c != 3:
                return
            P.dma(POOL, u_s[:, :, hh * 128:(hh + 1) * 128].rearrange("i p c -> p i c"), usth[hs],
                  reads=[f"usth{hs}"], writes=["u_s"], key=f"usth{hs}")

        NS = len(steps)
        loaded = -1
        deferred = []
        for n in range(NS + LAG):
            if n < NS:
                hh = steps[n][0]
                if hh > loaded:
                    head_loads(hh)
                    loaded = hh
                front(n)
            if n - LAG >= 0:
                back(n - LAG)
                while deferred and deferred[0][0] <= n - LAG:
                    deferred.pop(0)[1]()
        while deferred:
            deferred.pop(0)[1]()

        P.barrier(POOL, lambda e: e.memset(stat[:, 0:1], 0.0))
        AR.reset()
        wba = AR.alloc([8, D], BF16)
        wbb = AR.alloc([8, D], BF16)
        wo = AR.alloc([8, D], BF16)
        wst = [AR.alloc([1024], F32) for _ in range(8)]
        Ub = [AR.alloc([2048], BF16) for _ in range(2)]
        SMb = [AR.alloc([2048], BF16) for _ in range(2)]
        Xb = [AR.alloc([D], F32) for _ in range(2)]
        uT = [AR.alloc([16, 128], BF16) for _ in range(2)]
        t1 = AR.alloc([D], F32)
        t2 = AR.alloc([D], F32)
        t3 = AR.alloc([D], F32)
        mg = AR.alloc([D], BF16)
        mT = AR.alloc([8, 128], BF16)
        xn = AR.alloc([D], F32)
        junk = AR.alloc([D], BF16)
        Ob = [AR.alloc([D], F32) for _ in range(2)]
        load_weight_cols(lambda ch, c0, n: wba[:, ch, c0:c0 + n], w_ba, 0, 1024, "wba")
        load_weight_cols(lambda ch, c0, n: wbb[:, ch, c0:c0 + n], w_bb, 0, 1024, "wbb")
        load_weight_cols(lambda ch, c0, n: wo[:, ch, c0:c0 + n], w_out, 0, 1024, "wo")
        def p3_loads(i):
            s_ = i % 2
            P.dma(SP, Ub[s_], u_s[i], reads=["u_s"], writes=[f"U{s_}"], key=f"U{s_}")
            P.dma(SP, SMb[s_], sm_s[i], reads=["sm_s"], writes=[f"SM{s_}"], key=f"SM{s_}")
            P.dma(SP, Xb[s_], x_own[i * 128:(i + 1) * 128, :], writes=[f"X{s_}"], key=f"X{s_}")

        def p3_A(i):
            s_ = i % 2
            p3_loads(i)
            transpose_to(uT[s_][:, 0:8, :], Ub[s_][:, 0:1024], 8, f"U{s_}", f"uTa{s_}", 0)
            transpose_to(uT[s_][:, 8:16, :], Ub[s_][:, 1024:2048], 8, f"U{s_}", f"uTb{s_}", 1)

        def p3_B(i, hf):
            s_ = i % 2
            hsl = slice(hf * 512, (hf + 1) * 512)
            ba, bb = 2 + 2 * hf, 3 + 2 * hf
            for c in range(8):
                mm(bk[ba][:, :], uT[s_][:, c, :], wba[:, c, hsl], c == 0, c == 7, [f"uTa{s_}", "wba"], [BK[ba]])
            for c in range(8):
                mm(bk[bb][:, :], uT[s_][:, 8 + c, :], wbb[:, c, hsl], c == 0, c == 7, [f"uTb{s_}", "wbb"], [BK[bb]])
            tt(DVE, t1[:, hsl], bk[ba][:, :], SMb[s_][:, hsl], ALU.mult, [BK[ba], f"SM{s_}"], [f"t1{hf}"])
            tt(DVE, t2[:, hsl], bk[bb][:, :], SMb[s_][:, 1024 + hf * 512:1024 + (hf + 1) * 512], ALU.mult, [BK[bb], f"SM{s_}"], [f"t2{hf}"])
            tt(POOL, mg[:, hsl], t1[:, hsl], t2[:, hsl], ALU.add, [f"t1{hf}", f"t2{hf}"], [f"mg{hf}"])

        def p3_D(i):
            s_ = i % 2
            for c in range(8):
                tr(bkb[6][:, c * 128:(c + 1) * 128], mg[:, c * 128:(c + 1) * 128], [f"mg{c // 4}"], [BK[6]])
            act(mT, bkb[6][:, 0:1024].rearrange("p (c t) -> p c t", t=128), AF.Copy, [BK[6]], ["mT"])
            for hf in range(2):
                b = 7 if hf == 0 else 6
                hsl = slice(hf * 512, (hf + 1) * 512)
                for c in range(8):
                    mm(bk[b][:, :], mT[:, c, :], wo[:, c, hsl], c == 0, c == 7, ["mT", "wo"], [BK[b]])
                tt(DVE, t3[:, hsl], bk[b][:, :], gateB[:, hsl], ALU.mult, [BK[b]] + RES_GT, [f"t3{hf}"])
                tt(POOL, xn[:, hsl], t3[:, hsl], Xb[s_][:, hsl], ALU.add, [f"t3{hf}", f"X{s_}"], [f"xn{hf}"])
            sc = stat[:, 96 + i:97 + i]
            act(junk, xn, AF.Square, ["xn0", "xn1"], ["junk", "statc"], accum_out=sc)
            act(sc, sc, AF.Sqrt, ["statc"], ["statc"], scale=1.0 / D, bias=epsT[:, 0:1])
            P.op(DVE, lambda e, sc=sc: e.reciprocal(out=sc, in_=sc), ["statc"], ["statc"])
            stt(DVE, Ob[s_], xn, sc, gfinB[:], ALU.mult, ALU.mult, ["xn0", "xn1", "statc", "gfinB"], [f"O{s_}"])
            P.dma(POOL, out_d[i * 128:(i + 1) * 128, :], Ob[s_], reads=[f"O{s_}"], writes=[f"out{i}"], key=f"O{s_}")

        p3_A(0)
        for i in range(NT_OWN):
            if i + 1 < NT_OWN:
                p3_A(i + 1)
            p3_B(i, 0)
            p3_B(i, 1)
            p3_D(i)

        P.emit(st, final_deps_res=[f"out{i}" for i in range(NT_OWN)] + (["kT_s", "v_s", "qT_s", "sz_s", "sm_s", "u_s"] if dbg else []))
    return nc, P


_CACHE = {}


def _host_inputs(x, c, positions, w_ada, b_ada, g_norm, w_in, b_forget, lambda_q1, lambda_k1,
                 lambda_q2, lambda_k2, g_subln, w_branch_a, w_branch_b, w_out, g_final):
    f32 = np.float32
    inv64 = 10000.0 ** (-(np.arange(32, dtype=np.float64) / 32.0))
    inv_hi = inv64.astype(f32)
    inv_lo = (inv64 - inv_hi.astype(np.float64)).astype(f32)
    inv_freq = np.concatenate([inv_hi, inv_lo]).reshape(1, 64)
    ins = []
    s_i = np.arange(128)[:, None]
    t_i = np.arange(128)[None, :]
    for core in range(8):
        b, j = core // 4, core % 4
        xb = np.ascontiguousarray(x[b], dtype=f32)
        own_tiles = np.arange(NT_OWN) * 4 + j
        x_own = np.ascontiguousarray(xb.reshape(NT_ALL, 128, D)[own_tiles].reshape(NT_OWN * 128, D))
        pos_b = np.asarray(positions[b], dtype=np.int32).reshape(NT_ALL, 128)
        mA = np.zeros((128, 4, 128), f32)
        mB = np.full((128, 4, 128), NEG_BIG, f32)
        for d_ in range(4):
            if d_ < j:
                mA[:, d_, :] = 1.0
                mB[:, d_, :] = 0.0
            elif d_ == j:
                mA[:, d_, :] = ((s_i // 64) <= (t_i // 64)).astype(f32)
                mB[:, d_, :] = np.where(s_i <= t_i, 0.0, NEG_BIG).astype(f32)
        jsel = np.zeros((1, 4), f32)
        jsel[0, j] = 1.0
        ins.append({
            "x_all": xb, "x_own": x_own,
            "cT": np.ascontiguousarray(np.asarray(c[b], f32).reshape(8, 128).T),
            "pos_all": np.ascontiguousarray(pos_b.T), "pos_own": np.ascontiguousarray(pos_b[own_tiles].T),
            "w_ada": np.ascontiguousarray(w_ada[0], f32), "b_ada": np.ascontiguousarray(b_ada[0], f32).reshape(1, -1),
            "g_norm": np.asarray(g_norm[0], f32).reshape(1, -1), "w_in": np.ascontiguousarray(w_in[0], f32),
            "b_forget": np.asarray(b_forget[0], f32).reshape(1, -1),
            "lambda_q1": np.asarray(lambda_q1[0], f32).reshape(1, -1), "lambda_k1": np.asarray(lambda_k1[0], f32).reshape(1, -1),
            "lambda_q2": np.asarray(lambda_q2[0], f32).reshape(1, -1), "lambda_k2": np.asarray(lambda_k2[0], f32).reshape(1, -1),
            "g_subln": np.asarray(g_subln[0], f32).reshape(1, -1),
            "w_branch_a": np.ascontiguousarray(w_branch_a[0], f32), "w_branch_b": np.ascontiguousarray(w_branch_b[0], f32),
            "w_out": np.ascontiguousarray(w_out[0], f32), "g_final": np.asarray(g_final, f32).reshape(1, -1),
            "inv_freq": inv_freq,
            "maskA": mA.reshape(128, 512).astype(ml_dtypes.bfloat16), "maskB": mB.reshape(128, 512),
            "jsel": jsel,
        })
    return ins


def kernel(**inputs):
    inputs = {k: np.asarray(v) for k, v in inputs.items()}
    if "nc" not in _CACHE:
        _CACHE["nc"] = build_program(False)[0]
    nc = _CACHE["nc"]
    ins = _host_inputs(**inputs)
    res = run_bass_kernel_spmd(nc, ins, core_ids=list(range(8)))
    out = np.zeros((2, S_ALL, D), np.float32)
    o4 = out.reshape(2, NT_ALL, 128, D)
    for core in range(8):
        b, j = core // 4, core % 4
        y = np.asarray(res.results[core]["out_own"], dtype=np.float32).reshape(NT_OWN, 128, D)
        o4[b, np.arange(NT_OWN) * 4 + j] = y
    return out
```

```python
from contextlib import ExitStack
import math
import numpy as np
import ml_dtypes
import concourse.bass as bass
import concourse.mybir as mybir
from concourse.bass_utils import run_bass_kernel_spmd

F32 = mybir.dt.float32
BF16 = mybir.dt.bfloat16
I32 = mybir.dt.int32
AF = mybir.ActivationFunctionType
ALU = mybir.AluOpType

PE, ACT, DVE, POOL, SP = "tensor", "scalar", "vector", "gpsimd", "sync"
ENGS = [PE, ACT, DVE, POOL, SP]

S_ALL = 8192
NT_ALL = 64
NT_OWN = 16
D = 1024
N_IN = 10248
TWO_PI = 2.0 * math.pi
C1 = 6.28125
C2 = TWO_PI - C1
NEG_BIG = -30000.0


class Op:
    __slots__ = ("idx", "eng", "fn", "deps", "dma_key", "sem", "val", "signal")

    def __init__(self, idx, eng, fn, dma_key):
        self.idx, self.eng, self.fn, self.dma_key = idx, eng, fn, dma_key
        self.deps = set()
        self.sem = None
        self.val = 0
        self.signal = dma_key is not None


class Prog:
    def __init__(self, nc):
        self.nc = nc
        self.ops = []
        self.last_w = {}
        self.readers = {}
        self.barrier_idx = None

    def _key(self, idx):
        od = self.ops[idx]
        return od.dma_key if od.dma_key is not None else ("E", od.eng)

    def op(self, eng, fn, reads=(), writes=(), dma_key=None):
        o = Op(len(self.ops), eng, fn, dma_key)
        deps = {}

        def add(idx):
            k = self._key(idx)
            if deps.get(k, -1) < idx:
                deps[k] = idx

        if self.barrier_idx is not None:
            add(self.barrier_idx)
        for r in reads:
            w = self.last_w.get(r)
            if w is not None:
                add(w)
        for w_ in writes:
            w = self.last_w.get(w_)
            if w is not None:
                add(w)
            for rd in self.readers.get(w_, {}).values():
                add(rd)
        for d in deps.values():
            od = self.ops[d]
            if od.eng == PE and eng == PE and od.dma_key is None and dma_key is None:
                continue
            o.deps.add(d)
            od.signal = True
        me = dma_key if dma_key is not None else ("E", eng)
        for w_ in writes:
            self.last_w[w_] = o.idx
            self.readers[w_] = {}
        for r in reads:
            if r not in writes:
                self.readers.setdefault(r, {})[me] = o.idx
        self.ops.append(o)
        return o

    def dma(self, eng, out, in_, reads=(), writes=(), key=None):
        return self.op(eng, lambda e: e.dma_start(out=out, in_=in_), reads, writes, dma_key=key)

    def barrier(self, eng, fn):
        allres = set(self.last_w.keys()) | set(self.readers.keys())
        o = self.op(eng, fn, reads=(), writes=tuple(allres))
        self.barrier_idx = o.idx
        self.last_w = {}
        self.readers = {}
        return o

    def emit(self, stack, final_deps_res=()):
        nc = self.nc
        sems = {}
        cnt = {}
        fdeps = set()
        for r in final_deps_res:
            w = self.last_w.get(r)
            if w is not None:
                fdeps.add(w)
                self.ops[w].signal = True
        for o in self.ops:
            if not o.signal:
                continue
            name = ("e_" + o.eng) if o.dma_key is None else ("d_" + o.dma_key)
            inc = 1 if o.dma_key is None else 16
            cnt[name] = cnt.get(name, 0) + inc
            o.sem = name
            o.val = cnt[name]
        for name in cnt:
            sems[name] = stack.enter_context(nc.semaphore("s_" + name))
        self.sem_counts = cnt
        per_eng = {e: [] for e in ENGS}
        for o in self.ops:
            per_eng[o.eng].append(o)
        block = stack.enter_context(nc.Block())
        ops = self.ops

        def make(eng_name):
            def body(eng):
                waited = {}
                for o in per_eng[eng_name]:
                    need = {}
                    for d in o.deps:
                        od = ops[d]
                        if od.val > need.get(od.sem, 0):
                            need[od.sem] = od.val
                    for s, v in need.items():
                        if waited.get(s, 0) >= v:
                            continue
                        eng.wait_ge(sems[s], v)
                        waited[s] = v
                    ins = o.fn(eng)
                    if o.signal:
                        ins.then_inc(sems[o.sem], 1 if o.dma_key is None else 16)
                if eng_name == SP:
                    need = {}
                    for d in fdeps:
                        od = ops[d]
                        if od.val > need.get(od.sem, 0):
                            need[od.sem] = od.val
                    for s, v in need.items():
                        if waited.get(s, 0) < v:
                            eng.wait_ge(sems[s], v)
            return body

        for e in ENGS:
            if per_eng[e] or e == SP:
                getattr(block, e)(make(e))


class Arena:
    def __init__(self, t, nbytes):
        self.t = t
        self.nbytes = nbytes
        self.off = 0

    def reset(self):
        self.off = 0

    def alloc(self, shape_free, dt):
        esz = 4 if dt in (F32, I32) else 2
        n = int(np.prod(shape_free))
        nb = n * esz
        nb_al = (nb + 63) // 64 * 64
        assert self.off + nb_al <= self.nbytes, f"arena overflow {self.off + nb_al} > {self.nbytes}"
        a = self.off // 2
        ap = self.t[:, a:a + nb // 2]
        self.off += nb_al
        if dt != BF16:
            ap = ap.bitcast(dt)
        if len(shape_free) == 2:
            ap = ap.rearrange("p (a b) -> p a b", b=shape_free[1])
        elif len(shape_free) == 3:
            ap = ap.rearrange("p (a b c) -> p a b c", b=shape_free[1], c=shape_free[2])
        return ap


def build_program(dbg=False):
    nc = bass.Bass("TRN2", target_bir_lowering=False)

    def din(name, shape, dt=F32):
        return nc.dram_tensor(name, list(shape), dt, kind="ExternalInput").ap()

    x_all = din("x_all", [S_ALL, D])
    x_own = din("x_own", [NT_OWN * 128, D])
    cT_d = din("cT", [128, 8])
    pos_all_d = din("pos_all", [128, NT_ALL], I32)
    pos_own_d = din("pos_own", [128, NT_OWN], I32)
    w_ada = din("w_ada", [D, 3 * D])
    b_ada = din("b_ada", [1, 3 * D])
    g_norm = din("g_norm", [1, D])
    w_in = din("w_in", [D, N_IN])
    b_forget = din("b_forget", [1, 8])
    lq1 = din("lambda_q1", [1, 64])
    lk1 = din("lambda_k1", [1, 64])
    lq2 = din("lambda_q2", [1, 64])
    lk2 = din("lambda_k2", [1, 64])
    g_subln = din("g_subln", [1, 128])
    w_ba = din("w_branch_a", [D, D])
    w_bb = din("w_branch_b", [D, D])
    w_out = din("w_out", [D, D])
    g_final = din("g_final", [1, D])
    invf_d = din("inv_freq", [1, 64])
    maskA_d = din("maskA", [128, 512], BF16)
    maskB_d = din("maskB", [128, 512])
    jsel_d = din("jsel", [1, 4])
    out_d = nc.dram_tensor("out_own", [NT_OWN * 128, D], F32, kind="ExternalOutput").ap()

    sk = "ExternalOutput" if dbg else "Internal"
    kT_s = nc.dram_tensor("kT_s", [16, 128, S_ALL], BF16, kind=sk).ap()
    v_s = nc.dram_tensor("v_s", [16, 128, NT_ALL, 129], BF16, kind=sk).ap()
    qT_s = nc.dram_tensor("qT_s", [16, 128, NT_OWN * 128], BF16, kind=sk).ap()
    sz_s = nc.dram_tensor("sz_s", [16, 128, NT_OWN, 128], BF16, kind=sk).ap()
    sm_s = nc.dram_tensor("sm_s", [NT_OWN, 128, 2048], BF16, kind=sk).ap()
    u_s = nc.dram_tensor("u_s", [NT_OWN, 128, 2048], BF16, kind=sk).ap()
    negL_s = nc.dram_tensor("negL_s", [8, NT_OWN * 128], F32, kind=sk).ap()
    if dbg:
        dbgL = nc.dram_tensor("dbgL", [128, 512 + 128 + 16], F32, kind="ExternalOutput").ap()

    with ExitStack() as st:
        def sb(name, shape, dt):
            return st.enter_context(nc.sbuf_tensor(name, list(shape), dt))

        pairs = [st.enter_context(nc.psum_tensor(f"pp{i}", [128, 1024], F32)) for i in range(4)]
        bk = [pairs[i // 2][:, (i % 2) * 512:(i % 2 + 1) * 512] for i in range(8)]
        bkb = [b.bitcast(BF16) for b in bk]
        BK = [f"bk{i}" for i in range(8)]

        gsB = sb("gsB", [128, D], F32)
        shiftB = sb("shiftB", [128, D], F32)
        gateB = sb("gateB", [128, D], F32)
        gfinB = sb("gfinB", [128, D], F32)
        gsubB = sb("gsubB", [128, 128], F32)
        bfB = sb("bfB", [128, 8], F32)
        lamt = sb("lamt", [128, 8], F32)
        lv = sb("lv", [128, 4, 64], F32)
        identf = sb("identf", [128, 128], F32)
        ident = sb("ident", [128, 128], BF16)
        onesf = sb("onesf", [128, 128], F32)
        trif = sb("trif", [128, 128], F32)
        maskA = sb("maskA_s", [128, 4, 128], BF16)
        maskB = sb("maskB_s", [128, 4, 128], F32)
        jsel = sb("jsel_s", [128, 4], F32)
        invf = sb("invf", [128, 64], F32)
        cos_own = sb("cos_own", [128, NT_OWN, 32], F32)
        sin_own = sb("sin_own", [128, NT_OWN, 32], F32)
        Lall = sb("Lall", [128, NT_ALL, 8], F32)
        Lown = sb("Lown", [128, NT_OWN, 8], F32)
        stat = sb("stat", [128, 4 * 96], F32)
        epsT = sb("epsT", [128, 2], F32)
        ARENA_BYTES = 177 * 1024
        arena_t = sb("arena", [128, ARENA_BYTES // 2], BF16)
        AR = Arena(arena_t, ARENA_BYTES)

        P = Prog(nc)
        uid = [0]

        def R(name):
            uid[0] += 1
            return f"{name}#{uid[0]}"

        def act(out, in_, func, reads, writes, **kw):
            return P.op(ACT, lambda e: e.activation(out=out, in_=in_, func=func, **kw), reads, writes)

        def tt(eng, out, in0, in1, op, reads, writes):
            return P.op(eng, lambda e: e.tensor_tensor(out=out, in0=in0, in1=in1, op=op), reads, writes)

        def ts(eng, out, in0, s1, s2, op0, op1, reads, writes):
            if op1 is None:
                return P.op(eng, lambda e: e.tensor_single_scalar(out=out, in_=in0, scalar=s1, op=op0), reads, writes)
            return P.op(eng, lambda e: e.tensor_scalar(out=out, in0=in0, scalar1=s1, scalar2=s2, op0=op0, op1=op1), reads, writes)

        def stt(eng, out, in0, scalar, in1, op0, op1, reads, writes, accum_out=None):
            if accum_out is None:
                return P.op(eng, lambda e: e.scalar_tensor_tensor(out=out, in0=in0, scalar=scalar, in1=in1, op0=op0, op1=op1), reads, writes)
            return P.op(eng, lambda e: e.scalar_tensor_tensor(out=out, in0=in0, scalar=scalar, in1=in1, op0=op0, op1=op1, accum_out=accum_out), reads, writes)

        def cp(eng, out, in_, reads, writes):
            if eng == ACT:
                return P.op(eng, lambda e: e.activation(out=out, in_=in_, func=AF.Copy), reads, writes)
            return P.op(eng, lambda e: e.tensor_copy(out=out, in_=in_), reads, writes)

        def mm(out, lhsT, rhs, start, stop, reads, writes, skip=False):
            if skip:
                return P.op(PE, lambda e: e.matmul(out, lhsT=lhsT, rhs=rhs, start=start, stop=stop, skip_group_check=True), reads, writes)
            return P.op(PE, lambda e: e.matmul(out, lhsT=lhsT, rhs=rhs, start=start, stop=stop), reads, writes)

        def tr(out, in_, reads, writes, idn=None):
            idn = ident if idn is None else idn
            return P.op(PE, lambda e: e.transpose(out=out, in_=in_, identity=idn[:]), list(reads) + ["ident"], writes)

        TOP = (ARENA_BYTES - (8 * 4096 * 2 + 8 * 8 * 2 + 64 + 8 * 8 * 4 + 4 * 4096)) // 64 * 64
        AR.off = TOP
        wkv = AR.alloc([8, 4096], BF16)
        wf = AR.alloc([8, 8], BF16)
        wfst = AR.alloc([8, 8], F32)
        wst_top = [AR.alloc([1024], F32) for _ in range(4)]
        AR.reset()
        c_sb = AR.alloc([8], F32)
        c_sig = AR.alloc([8], F32)
        c_act = AR.alloc([8], F32)
        bada_r = AR.alloc([3 * D], F32)
        gn_r = AR.alloc([D], F32)
        mod_r = AR.alloc([3 * D], F32)
        gs_r = AR.alloc([D], F32)
        wada = [AR.alloc([3 * D], F32) for _ in range(2)]
        posi = AR.alloc([NT_ALL], I32)
        posf = AR.alloc([NT_ALL], F32)
        posoi = AR.alloc([NT_OWN], I32)
        posof = AR.alloc([NT_OWN], F32)
        ang = AR.alloc([32, 32], F32)
        rk = AR.alloc([32, 32], F32)
        rki = AR.alloc([32, 32], I32)
        rr = AR.alloc([32, 32], F32)
        rm = AR.alloc([32, 32], F32)
        phase0_end = AR.off

        def load_bc(dst, src, n, name):
            P.dma(SP, dst, src.partition_broadcast(128), writes=[name], key=R("ld"))

        P.dma(SP, c_sb, cT_d, writes=["c_sb"], key=R("ld"))
        P.dma(SP, bada_r[0:1, :], b_ada, writes=["bada_r"], key=R("ld"))
        P.dma(SP, gn_r[0:1, :], g_norm, writes=["gn_r"], key=R("ld"))
        load_bc(gfinB[:], g_final, D, "gfinB")
        load_bc(gsubB[:], g_subln, 128, "gsubB")
        load_bc(bfB[:], b_forget, 8, "bfB")
        for n_, src in enumerate((lq1, lk1, lq2, lk2)):
            load_bc(lv[:, n_, :], src, 64, f"lv{n_}")
        load_bc(invf[:], invf_d, 64, "invf")
        load_bc(jsel[:], jsel_d, 4, "jsel")
        P.dma(SP, maskA[:].rearrange("p a b -> p (a b)"), maskA_d, writes=["maskA"], key=R("ld"))
        P.dma(SP, maskB[:].rearrange("p a b -> p (a b)"), maskB_d, writes=["maskB"], key=R("ld"))
        P.dma(SP, posi, pos_all_d, writes=["Aposi"], key=R("ld"))
        P.dma(SP, posoi, pos_own_d, writes=["Oposi"], key=R("ld"))

        P.op(POOL, lambda e: e.memset(identf[:], 0.0), writes=["identf"])
        P.op(POOL, lambda e: e.affine_select(out=identf[:], in_=identf[:], pattern=[[-1, 128]], compare_op=ALU.not_equal, fill=1.0, base=0, channel_multiplier=1), reads=["identf"], writes=["identf"])
        P.op(POOL, lambda e: e.memset(trif[:], 1.0), writes=["trif"])
        P.op(POOL, lambda e: e.affine_select(out=trif[:], in_=trif[:], pattern=[[1, 128]], compare_op=ALU.is_ge, fill=0.0, base=0, channel_multiplier=-1), reads=["trif"], writes=["trif"])
        P.op(POOL, lambda e: e.memset(onesf[:], 1.0), writes=["onesf"])
        P.op(POOL, lambda e: e.memset(epsT[:, 0:1], 1e-6), writes=["epsT"])
        P.op(POOL, lambda e: e.memset(epsT[:, 1:2], 1e-5), writes=["epsT"])
        cp(DVE, ident[:], identf[:], ["identf"], ["ident"])

        act(c_sig, c_sb, AF.Sigmoid, ["c_sb"], ["c_sig"])
        tt(DVE, c_act, c_sb, c_sig, ALU.mult, ["c_sb", "c_sig"], ["c_act"])
        for ch in range(8):
            sl = ch % 2
            P.dma(SP, wada[sl], w_ada[ch * 128:(ch + 1) * 128, :], writes=[f"wada{sl}"], key=f"wada{sl}")
            for g in range(6):
                mm(bk[g][0:1, :], c_act[:, ch:ch + 1], wada[sl][:, g * 512:(g + 1) * 512], ch == 0, ch == 7,
                   ["c_act", f"wada{sl}"], [BK[g]])
        for g in range(6):
            tt(DVE, mod_r[0:1, g * 512:(g + 1) * 512], bk[g][0:1, :], bada_r[0:1, g * 512:(g + 1) * 512], ALU.add,
               [BK[g], "bada_r"], [f"mod{g}"])
        for hf in range(2):
            stt(DVE, gs_r[0:1, hf * 512:(hf + 1) * 512], mod_r[0:1, D + hf * 512:D + (hf + 1) * 512], 1.0,
                gn_r[0:1, hf * 512:(hf + 1) * 512], ALU.add, ALU.mult, [f"mod{2 + hf}", "gn_r"], [f"gs_r{hf}"])
        bi = 0
        for (row, off, dst, rn) in ((gs_r, 0, gsB, "gs_r"), (mod_r, 0, shiftB, "mod"), (mod_r, 2 * D, gateB, "mod")):
            for hf in range(2):
                b = 6 + (bi % 2)
                bi += 1
                rname = f"gs_r{hf}" if rn == "gs_r" else f"mod{(off // 512) + hf}"
                mm(bk[b][:, :], onesf[0:1, :], row[0:1, off + hf * 512:off + (hf + 1) * 512], True, True,
                   ["onesf", rname], [BK[b]])
                cp(ACT if hf else DVE, dst[:, hf * 512:(hf + 1) * 512], bk[b][:, :], [BK[b]], [dst.tensor.name + str(hf)] if False else [f"{id(dst)}_{hf}"])
        RES_GS = [f"{id(gsB)}_0", f"{id(gsB)}_1"]
        RES_SH = [f"{id(shiftB)}_0", f"{id(shiftB)}_1"]
        RES_GT = [f"{id(gateB)}_0", f"{id(gateB)}_1"]
        for n_ in range(2):
            stt(DVE, lv[:, 2 * n_, :], lv[:, 2 * n_, :], 1.0, lv[:, 2 * n_ + 1, :], ALU.mult, ALU.mult,
                [f"lv{2 * n_}", f"lv{2 * n_ + 1}"], [f"lv{2 * n_}", f"lamt{2 + n_}"], accum_out=lamt[:, 2 + n_:3 + n_])
            act(lamt[:, 4 + n_:5 + n_], lamt[:, 2 + n_:3 + n_], AF.Exp, [f"lamt{2 + n_}"], [f"lamt{4 + n_}"])
        tt(DVE, lamt[:, 0:1], lamt[:, 4:5], lamt[:, 5:6], ALU.subtract, ["lamt4", "lamt5"], ["lam"])
        ts(DVE, lamt[:, 0:1], lamt[:, 0:1], 0.2, None, ALU.add, None, ["lam"], ["lam"])
        ts(DVE, lamt[:, 1:2], lamt[:, 0:1], -1.0, None, ALU.mult, None, ["lam"], ["neglam"])

        wstn_ = 0
        for (col0_, dst0_) in ((1024, 0), (5120, 2048)):
            for ch_ in range(8):
                for c0_ in (0, 1024):
                    sl_ = wstn_ % 4
                    ceng_ = POOL if wstn_ % 4 == 3 else DVE
                    wstn_ += 1
                    P.dma(SP, wst_top[sl_], w_in[ch_ * 128:(ch_ + 1) * 128, col0_ + c0_:col0_ + c0_ + 1024],
                          writes=[f"wstT{sl_}"], key=f"wstT{sl_}")
                    P.op(ceng_, lambda e, o_=wkv[:, ch_, dst0_ + c0_:dst0_ + c0_ + 1024], i_=wst_top[sl_]: e.tensor_copy(out=o_, in_=i_),
                         [f"wstT{sl_}"], ["wkv"])
        P.dma(SP, wfst, w_in[:, 8192:8200].rearrange("(c p) n -> p c n", p=128), writes=["wfst"], key="wfst")
        P.op(POOL, lambda e: e.tensor_copy(out=wf, in_=wfst), ["wfst"], ["wf"])

        def rope_tables(pos_i, pos_f, nt, cos_dst, sin_dst, tag, posres):
            cp(DVE, pos_f, pos_i, [posres], [tag + "posf"])
            a_ = ang[:, 0:nt, :]
            k_ = rk[:, 0:nt, :]
            ki_ = rki[:, 0:nt, :]
            r_ = rr[:, 0:nt, :]
            m_ = rm[:, 0:nt, :]
            tt(DVE, a_, pos_f.unsqueeze(2).to_broadcast([128, nt, 32]), invf[:, 0:32].unsqueeze(1).to_broadcast([128, nt, 32]),
               ALU.mult, [tag + "posf", "invf"], ["ang"])
            tt(DVE, m_, pos_f.unsqueeze(2).to_broadcast([128, nt, 32]), invf[:, 32:64].unsqueeze(1).to_broadcast([128, nt, 32]),
               ALU.mult, [tag + "posf", "invf"], ["rm"])
            tt(DVE, a_, a_, m_, ALU.add, ["ang", "rm"], ["ang"])
            for which, dst in ((0, sin_dst), (1, cos_dst)):
                src = a_
                if which == 1:
                    ts(DVE, r_, a_, math.pi / 2, None, ALU.add, None, ["ang"], ["rr"])
                    src = r_
                ts(DVE, k_, src, 1.0 / TWO_PI, None, ALU.mult, None, ["ang", "rr"], ["rk"])
                cp(DVE, ki_, k_, ["rk"], ["rki"])
                cp(DVE, k_, ki_, ["rki"], ["rk"])
                stt(DVE, m_, k_, -C1, src, ALU.mult, ALU.add, ["rk", "ang", "rr"], ["rm"])
                stt(DVE, r_, k_, -C2, m_, ALU.mult, ALU.add, ["rk", "rm"], ["rr"])
                ts(DVE, m_, r_, math.pi, -TWO_PI, ALU.is_gt, ALU.mult, ["rr"], ["rm"])
                tt(DVE, r_, r_, m_, ALU.add, ["rr", "rm"], ["rr"])
                ts(DVE, m_, r_, -math.pi, TWO_PI, ALU.is_lt, ALU.mult, ["rr"], ["rm"])
                tt(DVE, r_, r_, m_, ALU.add, ["rr", "rm"], ["rr"])
                ts(DVE, r_, r_, math.pi, -math.pi, ALU.min, ALU.max, ["rr"], ["rr"])
                act(dst, r_, AF.Sin, ["rr"], [tag + ("cos" if which else "sin")])

        AR.off = phase0_end
        cos_all = AR.alloc([NT_ALL, 32], F32)
        sin_all = AR.alloc([NT_ALL, 32], F32)
        for hfi in range(2):
            sl32 = slice(hfi * 32, (hfi + 1) * 32)
            rope_tables(posi[:, sl32], posf[:, sl32], 32, cos_all[:, sl32, :], sin_all[:, sl32, :], "A", "Aposi")
        rope_tables(posoi, posof, NT_OWN, cos_own[:], sin_own[:], "O", "Oposi")
        assert AR.off <= TOP, (AR.off, TOP)
        P.barrier(POOL, lambda e: e.memset(stat[:, 0:1], 0.0))
        AR.reset()
        cos_all2 = AR.alloc([NT_ALL, 32], F32)
        sin_all2 = AR.alloc([NT_ALL, 32], F32)
        cp(DVE, cos_all2, cos_all, [], ["cosA"])
        cp(DVE, sin_all2, sin_all, [], ["sinA"])
        P.barrier(POOL, lambda e: e.memset(stat[:, 0:1], 0.0))
        cos_all, sin_all = cos_all2, sin_all2
        wst = None
        xt = [AR.alloc([D], F32) for _ in range(3)]
        junk = AR.alloc([D], BF16)
        hn = AR.alloc([D], F32)
        hb = [AR.alloc([D], BF16) for _ in range(3)]
        hT = [AR.alloc([8, 512], BF16) for _ in range(2)]
        ra = AR.alloc([512], F32)
        rb = AR.alloc([512], F32)
        kar = AR.alloc([D], BF16)
        kTa_st = AR.alloc([8, 512], BF16)
        kTb_st = AR.alloc([8, 512], BF16)
        v_st = [AR.alloc([16, 129], BF16) for _ in range(2)]
        ylog = AR.alloc([NT_ALL, 8], F32)
        lsb = AR.alloc([NT_ALL, 8], F32)
        cumt = AR.alloc([NT_ALL, 8], F32)
        totb = AR.alloc([NT_ALL, 8], F32)
        offs = AR.alloc([NT_ALL, 8], F32)
        assert AR.off <= TOP, (AR.off, TOP)

        wst_n = [0]

        def load_weight_cols(dst_fn, src, col0, ncols, reads_tag):
            for ch in range(8):
                for c0 in range(0, ncols, 1024):
                    n = min(1024, ncols - c0)
                    sl = wst_n[0] % len(wst)
                    ceng = (DVE, ACT, POOL, DVE, ACT)[wst_n[0] % 5]
                    wst_n[0] += 1
                    P.dma(SP, wst[sl][:, 0:n], src[ch * 128:(ch + 1) * 128, col0 + c0:col0 + c0 + n],
                          writes=[f"wst{sl}"], key=f"wst{sl}")
                    cp(ceng, dst_fn(ch, c0, n), wst[sl][:, 0:n], [f"wst{sl}"], [reads_tag])

        for s_ in range(2):
            P.op(POOL, lambda e, s_=s_: e.memset(v_st[s_][:, :, 128:129], 1.0), writes=[f"v_st{s_}"])

        gen_rot = [0]

        def gen_bank():
            b = 5 + (gen_rot[0] % 3)
            gen_rot[0] += 1
            return b

        def norm_tile(x_src_ap, xs, hs, statcol, shift_add=True):
            P.dma(SP, xt[xs], x_src_ap, writes=[f"xt{xs}"], key=f"xt{xs}")
            sc = stat[:, statcol:statcol + 1]
            act(junk, xt[xs], AF.Square, [f"xt{xs}"], ["junk", f"statc{xs}"], accum_out=sc)
            act(sc, sc, AF.Sqrt, [f"statc{xs}"], [f"statc{xs}"], scale=1.0 / D, bias=epsT[:, 0:1])
            P.op(DVE, lambda e: e.reciprocal(out=sc, in_=sc), [f"statc{xs}"], [f"statc{xs}"])
            stt(DVE, hn, xt[xs], sc, gsB[:], ALU.mult, ALU.mult, [f"xt{xs}", f"statc{xs}"] + RES_GS, ["hn"])
            tt(DVE, hb[hs], hn, shiftB[:], ALU.add, ["hn"] + RES_SH, [f"hb{hs}"])

        def transpose_to(dst3, src2, nchunks, src_res, dst_res, bank):
            for c in range(nchunks):
                tr(bkb[bank][:, c * 128:(c + 1) * 128], src2[:, c * 128:(c + 1) * 128], [src_res], [BK[bank]])
            act(dst3, bkb[bank][:, 0:nchunks * 128].rearrange("p (c t) -> p c t", t=128), AF.Copy, [BK[bank]], [dst_res])

        def rope_bank(bank, cos_t, sin_t, dst2, dst_res):
            v = bk[bank][:, :].rearrange("p (b h j) -> p b h j", h=2, j=32)
            t1 = v[:, :, 0, :]
            t2 = v[:, :, 1, :]
            cb = cos_t.unsqueeze(1).to_broadcast([128, 8, 32])
            sbb = sin_t.unsqueeze(1).to_broadcast([128, 8, 32])
            ra3 = ra[:, 0:256].rearrange("p (b j) -> p b j", j=32)
            rb3 = rb[:, 0:256].rearrange("p (b j) -> p b j", j=32)
            d4 = dst2.rearrange("p (b h j) -> p b h j", h=2, j=32)
            tt(DVE, ra3, t1, cb, ALU.mult, [BK[bank], "cs"], ["ra"])
            tt(DVE, rb3, t2, sbb, ALU.mult, [BK[bank], "cs"], ["rb"])
            tt(DVE, d4[:, :, 0, :], ra3, rb3, ALU.subtract, ["ra", "rb"], [dst_res])
            tt(DVE, ra3, t1, sbb, ALU.mult, [BK[bank], "cs"], ["ra"])
            tt(DVE, rb3, t2, cb, ALU.mult, [BK[bank], "cs"], ["rb"])
            tt(DVE, d4[:, :, 1, :], ra3, rb3, ALU.add, ["ra", "rb"], [dst_res])

        def stageN_all(t):
            norm_tile(x_all[t * 128:(t + 1) * 128, :], t % 3, t % 3, t)

        def stageA_all(t):
            g, t4 = t // 4, t % 4
            gs_ = g % 2
            transpose_to(hT[gs_][:, :, t4 * 128:(t4 + 1) * 128], hb[t % 3], 8, f"hb{t % 3}", f"hT{gs_}_{t4}", 0)

        def stageB_all(t):
            g, t4 = t // 4, t % 4
            gs_ = g % 2
            lhs = [hT[gs_][:, c, t4 * 128:(t4 + 1) * 128] for c in range(8)]
            for hf in range(2):
                b = 2 + hf
                for c in range(8):
                    mm(bk[b][:, :], lhs[c], wkv[:, c, hf * 512:(hf + 1) * 512], c == 0, c == 7, [f"hT{gs_}_{t4}", "wkv"], [BK[b]])
                rope_bank(b, cos_all[:, t, :], sin_all[:, t, :], kar[:, hf * 512:(hf + 1) * 512], "kar")
            for c in range(8):
                mm(bk[1][:, 0:8], lhs[c], wf[:, c, :], c == 0, c == 7, [f"hT{gs_}_{t4}", "wf"], [BK[1]])
            tt(DVE, ylog[:, t, :], bk[1][:, 0:8], bfB[:], ALU.add, [BK[1], "bfB"], ["ylog"])
            vs_ = t % 2
            for br, col0 in ((0, 1024), (1, 3072)):
                for hf in range(2):
                    b = gen_bank()
                    for c in range(8):
                        mm(bk[b][:, :], lhs[c], wkv[:, c, col0 + hf * 512:col0 + (hf + 1) * 512], c == 0, c == 7,
                           [f"hT{gs_}_{t4}", "wkv"], [BK[b]])
                    h0 = br * 8 + hf * 4
                    act(v_st[vs_][:, h0:h0 + 4, 0:128], bk[b][:, :].rearrange("p (h c) -> p h c", c=128), AF.Copy,
                        [BK[b]], [f"v_st{vs_}"])
            P.dma(POOL, v_s[:, :, t, :].rearrange("h p c -> p h c"), v_st[vs_], reads=[f"v_st{vs_}"], writes=["v_s"], key=f"v_st{vs_}")

        def stageC_all(t):
            g, t4 = t // 4, t % 4
            gs_ = g % 2
            transpose_to(kTa_st[:, :, t4 * 128:(t4 + 1) * 128], kar, 8, "kar", "kTa_st", 4)
            if t4 != 3:
                return
            for h in range(8):
                b = gen_bank()
                for c in range(8):
                    mm(bk[b][:, :], wkv[:, c, 2048 + h * 128:2048 + (h + 1) * 128], hT[gs_][:, c, :], c == 0, c == 7,
                       [f"hT{gs_}_0", f"hT{gs_}_1", f"hT{gs_}_2", f"hT{gs_}_3", "wkv"], [BK[b]])
                act(kTb_st[:, h, :], bk[b][:, :], AF.Copy, [BK[b]], ["kTb_st"])
            P.dma(POOL, kT_s[0:8, :, g * 512:(g + 1) * 512].rearrange("h p t -> p h t"), kTa_st, reads=["kTa_st"], writes=["kT_s"], key="kTa_st")
            P.dma(POOL, kT_s[8:16, :, g * 512:(g + 1) * 512].rearrange("h p t -> p h t"), kTb_st, reads=["kTb_st"], writes=["kT_s"], key="kTb_st")

        stageN_all(0)
        stageN_all(1)
        stageA_all(0)
        for t in range(NT_ALL):
            if t + 2 < NT_ALL:
                stageN_all(t + 2)
            if t + 1 < NT_ALL:
                stageA_all(t + 1)
            stageB_all(t)
            stageC_all(t)

        yl2 = ylog.rearrange("p a b -> p (a b)")
        ls2 = lsb.rearrange("p a b -> p (a b)")
        act(ls2, yl2, AF.Exp, ["ylog"], ["lsb"], scale=-1.0)
        ts(DVE, ls2, ls2, 1.0, None, ALU.add, None, ["lsb"], ["lsb"])
        act(ls2, ls2, AF.Ln, ["lsb"], ["lsb"])
        mm(bk[2][:, :], trif[:], ls2, True, True, ["trif", "lsb"], [BK[2]])
        mm(bk[3][:, :], onesf[:], ls2, True, True, ["onesf", "lsb"], [BK[3]])
        cp(DVE, cumt.rearrange("p a b -> p (a b)"), bk[2][:, :], [BK[2]], ["cumt"])
        cp(DVE, totb.rearrange("p a b -> p (a b)"), bk[3][:, :], [BK[3]], ["totb"])
        P.op(DVE, lambda e: e.memset(offs[:, 0, :], 0.0), writes=["offs"])
        for t in range(1, NT_ALL):
            tt(DVE, offs[:, t, :], offs[:, t - 1, :], totb[:, t - 1, :], ALU.add, ["offs", "totb"], ["offs"])
        tt(DVE, Lall[:], cumt, offs, ALU.add, ["cumt", "offs"], ["Lall"])
        L4 = Lall[:].rearrange("p (i d) h -> p i d h", d=4)
        ts(DVE, Lown[:], L4[:, :, 0, :], jsel[:, 0:1], None, ALU.mult, None, ["Lall", "jsel"], ["Lown"])
        for d_ in range(1, 4):
            stt(DVE, Lown[:], L4[:, :, d_, :], jsel[:, d_:d_ + 1], Lown[:], ALU.mult, ALU.add, ["Lall", "jsel", "Lown"], ["Lown"])
        mm(bk[4][:, 0:128], Lown[:].rearrange("p a b -> p (a b)"), identf[:], True, True, ["Lown", "identf"], [BK[4]])
        ts(DVE, ra[:, 0:128], bk[4][:, 0:128], -1.0, None, ALU.mult, None, [BK[4]], ["ra"])
        for i in range(NT_OWN):
            P.dma(POOL, negL_s[:, i * 128:(i + 1) * 128], ra[i * 8:(i + 1) * 8, 0:128], reads=["ra"], writes=["negL_s"], key=R("st"))
        if dbg:
            P.dma(POOL, dbgL[:, 0:512], Lall[:].rearrange("p a b -> p (a b)"), reads=["Lall"], key=R("st"))
            P.dma(POOL, dbgL[:, 512:640], Lown[:].rearrange("p a b -> p (a b)"), reads=["Lown"], key=R("st"))
            P.dma(POOL, dbgL[:, 640:642], lamt[:, 0:2], reads=["lam", "neglam"], key=R("st"))

        P.barrier(POOL, lambda e: e.memset(stat[:, 0:1], 0.0))
        AR.reset()
        wown = AR.alloc([8, 3072], BF16)
        wst = [AR.alloc([1024], F32) for _ in range(8)]
        xt = [AR.alloc([D], F32) for _ in range(3)]
        junk = AR.alloc([D], BF16)
        hn = AR.alloc([D], F32)
        hb = [AR.alloc([D], BF16) for _ in range(3)]
        hTo = AR.alloc([8, NT_OWN * 128], BF16)
        ra = AR.alloc([512], F32)
        rb = AR.alloc([512], F32)
        qar = AR.alloc([D], BF16)
        qTa_st = AR.alloc([8, 512], BF16)
        qTb_st = AR.alloc([8, 512], BF16)
        sgt = [AR.alloc([512], F32) for _ in range(2)]
        o_st = [AR.alloc([2048], BF16) for _ in range(2)]

        load_weight_cols(lambda ch, c0, n: wown[:, ch, c0:c0 + n], w_in, 0, 1024, "wown")
        load_weight_cols(lambda ch, c0, n: wown[:, ch, 1024 + c0:1024 + c0 + n], w_in, 4096, 1024, "wown")
        load_weight_cols(lambda ch, c0, n: wown[:, ch, 2048 + c0:2048 + c0 + n], w_in, 3072, 1024, "wown")
        osn = [0]

        def silu_out(col0, i, head0):
            os_ = osn[0] % 2
            osn[0] += 1
            for hf in range(2):
                b = gen_bank()
                for c in range(8):
                    mm(bk[b][:, :], hTo[:, c, i * 128:(i + 1) * 128], wown[:, c, col0 + hf * 512:col0 + (hf + 1) * 512],
                       c == 0, c == 7, [f"hTo{i}", "wown"], [BK[b]])
                act(sgt[hf], bk[b][:, :], AF.Sigmoid, [BK[b]], [f"sgt{hf}"])
                tt(DVE, o_st[os_][:, hf * 512:(hf + 1) * 512], bk[b][:, :], sgt[hf], ALU.mult, [BK[b], f"sgt{hf}"], [f"o_st{os_}"])
            P.dma(POOL, sz_s[head0:head0 + 8, :, i, :].rearrange("h p c -> p h c"),
                  o_st[os_][:, 0:1024].rearrange("p (h c) -> p h c", c=128), reads=[f"o_st{os_}"], writes=["sz_s"], key=f"o_st{os_}")

        def stageN_own(i):
            norm_tile(x_own[i * 128:(i + 1) * 128, :], i % 3, i % 3, 64 + i)

        def stageA_own(i):
            transpose_to(hTo[:, :, i * 128:(i + 1) * 128], hb[i % 3], 8, f"hb{i % 3}", f"hTo{i}", 0)

        def stageB_own(i):
            lhs = [hTo[:, c, i * 128:(i + 1) * 128] for c in range(8)]
            for hf in range(2):
                b = 2 + hf
                for c in range(8):
                    mm(bk[b][:, :], lhs[c], wown[:, c, hf * 512:(hf + 1) * 512], c == 0, c == 7, [f"hTo{i}", "wown"], [BK[b]])
                rope_bank(b, cos_own[:, i, :], sin_own[:, i, :], qar[:, hf * 512:(hf + 1) * 512], "qar")
            silu_out(2048, i, 0)

        def stageC_own(i):
            g, t4 = i // 4, i % 4
            transpose_to(qTa_st[:, :, t4 * 128:(t4 + 1) * 128], qar, 8, "qar", "qTa_st", 4)
            if t4 != 3:
                return
            for h in range(8):
                b = gen_bank()
                for c in range(8):
                    mm(bk[b][:, :], wown[:, c, 1024 + h * 128:1024 + (h + 1) * 128], hTo[:, c, g * 512:(g + 1) * 512], c == 0, c == 7,
                       [f"hTo{4 * g}", f"hTo{4 * g + 1}", f"hTo{4 * g + 2}", f"hTo{4 * g + 3}", "wown"], [BK[b]])
                act(qTb_st[:, h, :], bk[b][:, :], AF.Copy, [BK[b]], ["qTb_st"])
            P.dma(POOL, qT_s[0:8, :, g * 512:(g + 1) * 512].rearrange("h p t -> p h t"), qTa_st, reads=["qTa_st"], writes=["qT_s"], key="qTa_st")
            P.dma(POOL, qT_s[8:16, :, g * 512:(g + 1) * 512].rearrange("h p t -> p h t"), qTb_st, reads=["qTb_st"], writes=["qT_s"], key="qTb_st")

        stageN_own(0)
        stageN_own(1)
        stageN_own(2)
        stageA_own(0)
        stageA_own(1)
        for i in range(NT_OWN):
            if i + 3 < NT_OWN:
                stageN_own(i + 3)
            if i + 2 < NT_OWN:
                stageA_own(i + 2)
            stageB_own(i)
            stageC_own(i)

        def gen_bank():
            b = 2 + (gen_rot[0] % 6)
            gen_rot[0] += 1
            return b

        load_weight_cols(lambda ch, c0, n: wown[:, ch, c0:c0 + n], w_in, 7168, 1024, "wown")
        load_weight_cols(lambda ch, c0, n: wown[:, ch, 1024 + c0:1024 + c0 + n], w_in, 8200, 2048, "wown")
        for i in range(NT_OWN):
            silu_out(0, i, 8)
            os_ = osn[0] % 2
            osn[0] += 1
            for q4 in range(4):
                b = gen_bank()
                for c in range(8):
                    mm(bk[b][:, :], hTo[:, c, i * 128:(i + 1) * 128], wown[:, c, 1024 + q4 * 512:1024 + (q4 + 1) * 512],
                       c == 0, c == 7, [f"hTo{i}", "wown"], [BK[b]])
                act(o_st[os_][:, q4 * 512:(q4 + 1) * 512], bk[b][:, :], AF.Sigmoid, [BK[b]], [f"o_st{os_}"])
            P.dma(POOL, sm_s[i], o_st[os_], reads=[f"o_st{os_}"], writes=["sm_s"], key=f"o_st{os_}")

        P.barrier(POOL, lambda e: e.memset(stat[:, 0:1], 0.0))
        AR.reset()
        Kb = [AR.alloc([S_ALL], BF16) for _ in range(2)]
        Vb = [AR.alloc([NT_ALL, 129], BF16) for _ in range(2)]
        Qb = [AR.alloc([NT_OWN * 128], BF16) for _ in range(2)]
        SZb = [AR.alloc([NT_OWN, 128], BF16) for _ in range(2)]
        NPB = 7
        LAG = 5
        Pb = [AR.alloc([2, 512], BF16) for _ in range(NPB)]
        tmpb = [AR.alloc([512], F32) for _ in range(5)]
        FqB = [AR.alloc([NT_OWN * 128], F32) for _ in range(2)]
        dg = [AR.alloc([128], F32) for _ in range(2)]
        usth = [AR.alloc([NT_OWN, 128], BF16) for _ in range(2)]
        obuf2 = [AR.alloc([NT_OWN, 128], F32) for _ in range(2)]
        accS = AR.alloc([8, 129], F32)
        eo1 = AR.alloc([128], F32)
        ejk = AR.alloc([128], F32)
        est = AR.alloc([16], F32)
        mscol2 = [AR.alloc([NT_OWN], F32) for _ in range(2)]

        SCALE_A = 64 ** -0.5
        SCALE_B = 128 ** -0.5

        def build_fq(h8):
            fs = h8 % 2
            P.dma(SP, FqB[fs], negL_s[h8:h8 + 1, :].partition_broadcast(128), reads=["negL_s"], writes=[f"FqB{fs}"], key=f"FqB{fs}")

        steps = []
        ccn = 0
        for hh in range(16):
            for c in range(4):
                nk = 16 * c + 16
                for k in range(nk):
                    steps.append((hh, c, k, ccn, k == 0, k == nk - 1))
                ccn += 1

        def head_loads(hh):
            hs = hh % 2
            P.dma(SP, Kb[hs], kT_s[hh], reads=["kT_s"], writes=[f"K{hs}"], key=f"K{hs}")
            P.dma(SP, Vb[hs], v_s[hh], reads=["v_s"], writes=[f"V{hs}"], key=f"V{hs}")
            P.dma(SP, Qb[hs], qT_s[hh], reads=["qT_s"], writes=[f"Q{hs}"], key=f"Q{hs}")
            P.dma(SP, SZb[hs], sz_s[hh], reads=["sz_s"], writes=[f"SZ{hs}"], key=f"SZ{hs}")
            if hh == 7:
                build_fq(0)
            elif 8 <= hh < 15:
                build_fq(hh - 8 + 1)

        def amin_of(c, k):
            return max(0, -(-(k - 16 * c - 3) // 4))

        def front(n):
            hh, c, k, cc, first, last = steps[n]
            is_a = hh < 8
            hs = hh % 2
            h8 = hh % 8
            a_min = amin_of(c, k)
            col0 = a_min * 128
            ps = n % NPB
            if is_a:
                sp = n % 2
                s3 = pairs[sp][:, :].rearrange("p (m q) -> p m q", m=2)
                for m in range(2):
                    pr = slice(m * 64, (m + 1) * 64)
                    mm(bk[2 * sp + m][:, col0:512], Kb[hs][pr, k * 128:(k + 1) * 128],
                       Qb[hs][pr, c * 512 + col0:(c + 1) * 512], True, True, [f"K{hs}", f"Q{hs}"], [BK[2 * sp + m]])
                act(Pb[ps][:, :, col0:512], s3[:, :, col0:512], AF.Exp, [BK[2 * sp], BK[2 * sp + 1]], [f"P{ps}"], scale=SCALE_A)
                for a in range(a_min, 4):
                    d_ = k - 16 * c - 4 * a
                    if 0 <= d_ <= 3:
                        blk = slice(a * 128, (a + 1) * 128)
                        tt(POOL, Pb[ps][:, :, blk], Pb[ps][:, :, blk], maskA[:, d_, :].unsqueeze(1).to_broadcast([128, 2, 128]),
                           ALU.mult, [f"P{ps}", "maskA"], [f"P{ps}"])
            else:
                sbk = n % 4
                ts_ = n % 5
                fs = h8 % 2
                mm(bk[sbk][:, col0:512], Kb[hs][:, k * 128:(k + 1) * 128], Qb[hs][:, c * 512 + col0:(c + 1) * 512], True, True,
                   [f"K{hs}", f"Q{hs}"], [BK[sbk]])
                stt(DVE, tmpb[ts_][:, col0:512], bk[sbk][:, col0:512], SCALE_B, FqB[fs][:, c * 512 + col0:(c + 1) * 512],
                    ALU.mult, ALU.add, [BK[sbk], f"FqB{fs}"], [f"tmp{ts_}"])
                for a in range(a_min, 4):
                    d_ = k - 16 * c - 4 * a
                    if 0 <= d_ <= 3:
                        blk = slice(a * 128, (a + 1) * 128)
                        tt(POOL, tmpb[ts_][:, blk], tmpb[ts_][:, blk], maskB[:, d_, :], ALU.add, [f"tmp{ts_}", "maskB"], [f"tmp{ts_}"])
                act(Pb[ps][:, 0, col0:512], tmpb[ts_][:, col0:512], AF.Exp, [f"tmp{ts_}", "Lall"], [f"P{ps}"],
                    bias=Lall[:, k, h8:h8 + 1], scale=1.0)

        def acc_a(m, a):
            b = 4 + 2 * m + a // 2
            o_ = (a % 2) * 129
            return b, bk[b][:, o_:o_ + 129]

        def acc_b(cc, a):
            b = 4 + 2 * (cc % 2) + a // 2
            o_ = (a % 2) * 129
            return b, bk[b][:, o_:o_ + 129]

        def zero_banks(banks):
            for b in banks:
                P.op(DVE, lambda e, b=b: e.memset(bk[b][:, 0:258], 0.0), writes=[BK[b]])

        def next_is_a(n):
            return n + 1 < NS and steps[n + 1][0] < 8

        def back(n):
            hh, c, k, cc, first, last = steps[n]
            is_a = hh < 8
            hs = hh % 2
            a_min = amin_of(c, k)
            ps = n % NPB
            if n == 0:
                zero_banks((4, 5, 6, 7))
            if is_a:
                obuf, mscol = obuf2[hs], mscol2[hs]
                OB, MS = f"obuf{hs}", f"mscol{hs}"
                for a in range(a_min, 4):
                    blk = slice(a * 128, (a + 1) * 128)
                    for m in range(2):
                        b, acc = acc_a(m, a)
                        mm(acc, Pb[ps][:, m, blk], Vb[hs][:, k, :], False, False, [f"P{ps}", f"V{hs}"], [BK[b]], skip=True)
                if not last:
                    return
                for j, b in enumerate((4, 5, 6, 7)):
                    src = bk[b][:, 0:258].rearrange("p (a c) -> p a c", c=129)
                    cp(DVE, accS[:, 2 * j:2 * j + 2, :], src, [BK[b]], [f"accS{j}"])
                if n + 1 < NS:
                    zero_banks((4, 5, 6, 7))
                for a in range(4):
                    i = 4 * c + a
                    j1, j2 = a // 2, 2 + a // 2
                    acc1 = accS[:, 2 * j1 + a % 2, :]
                    acc2 = accS[:, 2 * j2 + a % 2, :]
                    P.op(DVE, lambda e, acc1=acc1: e.reciprocal(out=est[:, 0:1], in_=acc1[:, 128:129]), [f"accS{j1}"], ["est0"])
                    P.op(DVE, lambda e, acc2=acc2: e.reciprocal(out=est[:, 1:2], in_=acc2[:, 128:129]), [f"accS{j2}"], ["est1"])
                    tt(DVE, est[:, 1:2], est[:, 1:2], lamt[:, 1:2], ALU.mult, ["est1", "neglam"], ["est1"])
                    ts(DVE, eo1, acc1[:, 0:128], est[:, 0:1], None, ALU.mult, None, [f"accS{j1}", "est0"], ["eo1"])
                    stt(DVE, obuf[:, i, :], acc2[:, 0:128], est[:, 1:2], eo1, ALU.mult, ALU.add, [f"accS{j2}", "est1", "eo1"], [OB])
                    stt(DVE, ejk, obuf[:, i, :], 1.0, obuf[:, i, :], ALU.mult, ALU.mult, [OB], ["ejk", MS], accum_out=mscol[:, i:i + 1])
                if c != 3:
                    return

                def head_epi(hh=hh, hs=hs, obuf=obuf, mscol=mscol, OB=OB, MS=MS):
                    act(mscol, mscol, AF.Sqrt, [MS], [MS], scale=1.0 / 128, bias=epsT[:, 1:2])
                    P.op(DVE, lambda e: e.reciprocal(out=mscol, in_=mscol), [MS], [MS])
                    ts(DVE, mscol, mscol, 0.8, None, ALU.mult, None, [MS], [MS])
                    tt(DVE, obuf, obuf, mscol.unsqueeze(2).to_broadcast([128, NT_OWN, 128]), ALU.mult, [OB, MS], [OB])
                    tt(DVE, obuf, obuf, gsubB[:].unsqueeze(1).to_broadcast([128, NT_OWN, 128]), ALU.mult, [OB, "gsubB"], [OB])
                    tt(DVE, usth[hs], obuf, SZb[hs], ALU.mult, [OB, f"SZ{hs}"], [f"usth{hs}"])
                    P.dma(POOL, u_s[:, :, hh * 128:(hh + 1) * 128].rearrange("i p c -> p i c"), usth[hs],
                          reads=[f"usth{hs}"], writes=["u_s"], key=f"usth{hs}")
                deferred.append((n + 28, head_epi))
                return
            for a in range(a_min, 4):
                blk = slice(a * 128, (a + 1) * 128)
                b, acc = acc_b(cc, a)
                mm(acc, Pb[ps][:, 0, blk], Vb[hs][:, k, :], False, False, [f"P{ps}", f"V{hs}"], [BK[b]], skip=True)
            if not last:
                return
            if n + 1 < NS:
                sn_ = (cc + 1) % 2
                zero_banks((4 + 2 * sn_, 5 + 2 * sn_))
            for a in range(4):
                i = 4 * c + a
                b1, acc1 = acc_b(cc, a)
                P.op(DVE, lambda e, acc1=acc1: e.reciprocal(out=est[:, 0:1], in_=acc1[:, 128:129]), [BK[b1]], ["est0"])
                stt(DVE, usth[hs][:, i, :], acc1[:, 0:128], est[:, 0:1], SZb[hs][:, i, :], ALU.mult, ALU.mult,
                    [BK[b1], "est0", f"SZ{hs}"], [f"usth{hs}"])
            if c != 3:
                return
            P.dma(POOL, u_s[:, :, hh * 128:(hh + 1) * 128].rearrange("i p c -> p i c"), usth[hs],
                  reads=[f"usth{hs}"], writes=["u_s"], key=f"usth{hs}")

        NS = len(steps)
        loaded = -1
        deferred = []
        for n in range(NS + LAG):
            if n < NS:
                hh = steps[n][0]
                if hh > loaded:
                    head_loads(hh)
                    loaded = hh
                front(n)
            if n - LAG >= 0:
                back(n - LAG)
                while deferred and deferred[0][0] <= n - LAG:
                    deferred.pop(0)[1]()
        while deferred:
            deferred.pop(0)[1]()

        P.barrier(POOL, lambda e: e.memset(stat[:, 0:1], 0.0))
        AR.reset()
        wba = AR.alloc([8, D], BF16)
        wbb = AR.alloc([8, D], BF16)
        wo = AR.alloc([8, D], BF16)
        wst = [AR.alloc([1024], F32) for _ in range(8)]
        Ub = [AR.alloc([2048], BF16) for _ in range(2)]
        SMb = [AR.alloc([2048], BF16) for _ in range(2)]
        Xb = [AR.alloc([D], F32) for _ in range(2)]
        uT = [AR.alloc([16, 128], BF16) for _ in range(2)]
        t1 = AR.alloc([D], F32)
        t2 = AR.alloc([D], F32)
        t3 = AR.alloc([D], F32)
        mg = AR.alloc([D], BF16)
        mT = AR.alloc([8, 128], BF16)
        xn = AR.alloc([D], F32)
        junk = AR.alloc([D], BF16)
        Ob = [AR.alloc([D], F32) for _ in range(2)]
        load_weight_cols(lambda ch, c0, n: wba[:, ch, c0:c0 + n], w_ba, 0, 1024, "wba")
        load_weight_cols(lambda ch, c0, n: wbb[:, ch, c0:c0 + n], w_bb, 0, 1024, "wbb")
        load_weight_cols(lambda ch, c0, n: wo[:, ch, c0:c0 + n], w_out, 0, 1024, "wo")
        def p3_loads(i):
            s_ = i % 2
            P.dma(SP, Ub[s_], u_s[i], reads=["u_s"], writes=[f"U{s_}"], key=f"U{s_}")
            P.dma(SP, SMb[s_], sm_s[i], reads=["sm_s"], writes=[f"SM{s_}"], key=f"SM{s_}")
            P.dma(SP, Xb[s_], x_own[i * 128:(i + 1) * 128, :], writes=[f"X{s_}"], key=f"X{s_}")

        def p3_A(i):
            s_ = i % 2
            p3_loads(i)
            transpose_to(uT[s_][:, 0:8, :], Ub[s_][:, 0:1024], 8, f"U{s_}", f"uTa{s_}", 0)
            transpose_to(uT[s_][:, 8:16, :], Ub[s_][:, 1024:2048], 8, f"U{s_}", f"uTb{s_}", 1)

        def p3_B(i, hf):
            s_ = i % 2
            hsl = slice(hf * 512, (hf + 1) * 512)
            ba, bb = 2 + 2 * hf, 3 + 2 * hf
            for c in range(8):
                mm(bk[ba][:, :], uT[s_][:, c, :], wba[:, c, hsl], c == 0, c == 7, [f"uTa{s_}", "wba"], [BK[ba]])
            for c in range(8):
                mm(bk[bb][:, :], uT[s_][:, 8 + c, :], wbb[:, c, hsl], c == 0, c == 7, [f"uTb{s_}", "wbb"], [BK[bb]])
            tt(DVE, t1[:, hsl], bk[ba][:, :], SMb[s_][:, hsl], ALU.mult, [BK[ba], f"SM{s_}"], [f"t1{hf}"])
            tt(DVE, t2[:, hsl], bk[bb][:, :], SMb[s_][:, 1024 + hf * 512:1024 + (hf + 1) * 512], ALU.mult, [BK[bb], f"SM{s_}"], [f"t2{hf}"])
            tt(POOL, mg[:, hsl], t1[:, hsl], t2[:, hsl], ALU.add, [f"t1{hf}", f"t2{hf}"], [f"mg{hf}"])

        def p3_D(i):
            s_ = i % 2
            for c in range(8):
                tr(bkb[6][:, c * 128:(c + 1) * 128], mg[:, c * 128:(c + 1) * 128], [f"mg{c // 4}"], [BK[6]])
            act(mT, bkb[6][:, 0:1024].rearrange("p (c t) -> p c t", t=128), AF.Copy, [BK[6]], ["mT"])
            for hf in range(2):
                b = 7 if hf == 0 else 6
                hsl = slice(hf * 512, (hf + 1) * 512)
                for c in range(8):
                    mm(bk[b][:, :], mT[:, c, :], wo[:, c, hsl], c == 0, c == 7, ["mT", "wo"], [BK[b]])
                tt(DVE, t3[:, hsl], bk[b][:, :], gateB[:, hsl], ALU.mult, [BK[b]] + RES_GT, [f"t3{hf}"])
                tt(POOL, xn[:, hsl], t3[:, hsl], Xb[s_][:, hsl], ALU.add, [f"t3{hf}", f"X{s_}"], [f"xn{hf}"])
            sc = stat[:, 96 + i:97 + i]
            act(junk, xn, AF.Square, ["xn0", "xn1"], ["junk", "statc"], accum_out=sc)
            act(sc, sc, AF.Sqrt, ["statc"], ["statc"], scale=1.0 / D, bias=epsT[:, 0:1])
            P.op(DVE, lambda e, sc=sc: e.reciprocal(out=sc, in_=sc), ["statc"], ["statc"])
            stt(DVE, Ob[s_], xn, sc, gfinB[:], ALU.mult, ALU.mult, ["xn0", "xn1", "statc", "gfinB"], [f"O{s_}"])
            P.dma(POOL, out_d[i * 128:(i + 1) * 128, :], Ob[s_], reads=[f"O{s_}"], writes=[f"out{i}"], key=f"O{s_}")

        p3_A(0)
        for i in range(NT_OWN):
            if i + 1 < NT_OWN:
                p3_A(i + 1)
            p3_B(i, 0)
            p3_B(i, 1)
            p3_D(i)

        P.emit(st, final_deps_res=[f"out{i}" for i in range(NT_OWN)] + (["kT_s", "v_s", "qT_s", "sz_s", "sm_s", "u_s"] if dbg else []))
    return nc, P


_CACHE = {}


def _host_inputs(x, c, positions, w_ada, b_ada, g_norm, w_in, b_forget, lambda_q1, lambda_k1,
                 lambda_q2, lambda_k2, g_subln, w_branch_a, w_branch_b, w_out, g_final):
    f32 = np.float32
    inv64 = 10000.0 ** (-(np.arange(32, dtype=np.float64) / 32.0))
    inv_hi = inv64.astype(f32)
    inv_lo = (inv64 - inv_hi.astype(np.float64)).astype(f32)
    inv_freq = np.concatenate([inv_hi, inv_lo]).reshape(1, 64)
    ins = []
    s_i = np.arange(128)[:, None]
    t_i = np.arange(128)[None, :]
    for core in range(8):
        b, j = core // 4, core % 4
        xb = np.ascontiguousarray(x[b], dtype=f32)
        own_tiles = np.arange(NT_OWN) * 4 + j
        x_own = np.ascontiguousarray(xb.reshape(NT_ALL, 128, D)[own_tiles].reshape(NT_OWN * 128, D))
        pos_b = np.asarray(positions[b], dtype=np.int32).reshape(NT_ALL, 128)
        mA = np.zeros((128, 4, 128), f32)
        mB = np.full((128, 4, 128), NEG_BIG, f32)
        for d_ in range(4):
            if d_ < j:
                mA[:, d_, :] = 1.0
                mB[:, d_, :] = 0.0
            elif d_ == j:
                mA[:, d_, :] = ((s_i // 64) <= (t_i // 64)).astype(f32)
                mB[:, d_, :] = np.where(s_i <= t_i, 0.0, NEG_BIG).astype(f32)
        jsel = np.zeros((1, 4), f32)
        jsel[0, j] = 1.0
        ins.append({
            "x_all": xb, "x_own": x_own,
            "cT": np.ascontiguousarray(np.asarray(c[b], f32).reshape(8, 128).T),
            "pos_all": np.ascontiguousarray(pos_b.T), "pos_own": np.ascontiguousarray(pos_b[own_tiles].T),
            "w_ada": np.ascontiguousarray(w_ada[0], f32), "b_ada": np.ascontiguousarray(b_ada[0], f32).reshape(1, -1),
            "g_norm": np.asarray(g_norm[0], f32).reshape(1, -1), "w_in": np.ascontiguousarray(w_in[0], f32),
            "b_forget": np.asarray(b_forget[0], f32).reshape(1, -1),
            "lambda_q1": np.asarray(lambda_q1[0], f32).reshape(1, -1), "lambda_k1": np.asarray(lambda_k1[0], f32).reshape(1, -1),
            "lambda_q2": np.asarray(lambda_q2[0], f32).reshape(1, -1), "lambda_k2": np.asarray(lambda_k2[0], f32).reshape(1, -1),
            "g_subln": np.asarray(g_subln[0], f32).reshape(1, -1),
            "w_branch_a": np.ascontiguousarray(w_branch_a[0], f32), "w_branch_b": np.ascontiguousarray(w_branch_b[0], f32),
            "w_out": np.ascontiguousarray(w_out[0], f32), "g_final": np.asarray(g_final, f32).reshape(1, -1),
            "inv_freq": inv_freq,
            "maskA": mA.reshape(128, 512).astype(ml_dtypes.bfloat16), "maskB": mB.reshape(128, 512),
            "jsel": jsel,
        })
    return ins


def kernel(**inputs):
    inputs = {k: np.asarray(v) for k, v in inputs.items()}
    if "nc" not in _CACHE:
        _CACHE["nc"] = build_program(False)[0]
    nc = _CACHE["nc"]
    ins = _host_inputs(**inputs)
    res = run_bass_kernel_spmd(nc, ins, core_ids=list(range(8)))
    out = np.zeros((2, S_ALL, D), np.float32)
    o4 = out.reshape(2, NT_ALL, 128, D)
    for core in range(8):
        b, j = core // 4, core % 4
        y = np.asarray(res.results[core]["out_own"], dtype=np.float32).reshape(NT_OWN, 128, D)
        o4[b, np.arange(NT_OWN) * 4 + j] = y
    return out
```
